# Optimizing a Trainium2 kernel written in Bass

```python
import math
import jax, jax.numpy as jnp
from jax import lax
import numpy as np

D_MODEL = 4096
BATCH = 8
SEQ = 2048
DEPTH = 1

MIX_WIDTH = D_MODEL
POOL_WIDTH = MIX_WIDTH // 2
SSM_WIDTH = MIX_WIDTH - POOL_WIDTH
POOL_WINDOWS = (2, 4, 8, 16)
POOL_GROUPS = len(POOL_WINDOWS)
POOL_GROUP_WIDTH = POOL_WIDTH // POOL_GROUPS
SSM_GROUP_CH = 16
SSM_GROUPS = SSM_WIDTH // SSM_GROUP_CH
SSM_STATE = 64
D_FF = ((8 * D_MODEL // 3 + 255) // 256) * 256
DT_MIN = 1e-3
DT_MAX = 1e-1
NORM_EPS = 1e-6

kernel_name = "macaron_pool_s5_hybrid_block"


def rms_norm(x, g):
    xf = x.astype(jnp.float32)
    y = xf * lax.rsqrt(jnp.mean(xf * xf, axis=-1, keepdims=True) + NORM_EPS)
    return (y * g.astype(jnp.float32)).astype(x.dtype)


def swiglu_ffn(h, w_gate, w_up, w_down):
    return (jax.nn.silu(h @ w_gate) * (h @ w_up)) @ w_down


def causal_multiscale_pool(z, w_pool, pool_scale):
    b, s, _ = z.shape
    zf = z.astype(jnp.float32).reshape(b, s, POOL_GROUPS, POOL_GROUP_WIDTH)
    cs = jnp.cumsum(zf, axis=1)
    t = jnp.arange(s)
    diffs = []
    for g, w in enumerate(POOL_WINDOWS):
        c = cs[:, :, g]
        lagged = jnp.pad(c[:, : s - w], ((0, 0), (w, 0), (0, 0)))
        cnt = jnp.minimum(t + 1, w).astype(jnp.float32)[None, :, None]
        diffs.append((c - lagged) / cnt - zf[:, :, g])
    d = jnp.stack(diffs, axis=2).astype(z.dtype)
    out = jnp.einsum("bsgc,gcd->bsgd", d, w_pool).reshape(b, s, POOL_WIDTH)
    return out * pool_scale


def _ssm_combine(e1, e2):
    a1, b1 = e1
    a2, b2 = e2
    return a1 * a2, a2 * b1 + b2


def s5_mixer(z, lam_re, lam_im, log_dt, b_re, b_im, c_re, c_im, d_skip, w_glu, b_glu):
    b, s, _ = z.shape
    u = z.astype(jnp.float32).reshape(b, s, SSM_GROUPS, SSM_GROUP_CH)
    lam = lax.complex(lam_re.astype(jnp.float32), lam_im.astype(jnp.float32))
    dt = jnp.exp(log_dt.astype(jnp.float32))[:, None]
    lam_bar = jnp.exp(lam * dt)
    b_mat = lax.complex(b_re.astype(jnp.float32), b_im.astype(jnp.float32))
    b_bar = ((lam_bar - 1.0) / lam)[:, :, None] * b_mat
    c_mat = lax.complex(c_re.astype(jnp.float32), c_im.astype(jnp.float32))
    bu = jnp.einsum("bsgh,gph->bsgp", u.astype(jnp.complex64), b_bar)
    a = jnp.broadcast_to(lam_bar, bu.shape)
    _, states = lax.associative_scan(_ssm_combine, (a, bu), axis=1)
    y = jnp.einsum("ghp,bsgp->bsgh", c_mat, states).real
    y = y + d_skip.astype(jnp.float32).reshape(SSM_GROUPS, SSM_GROUP_CH) * u
    y = jax.nn.gelu(y.reshape(b, s, SSM_WIDTH)).astype(z.dtype)
    return y * jax.nn.sigmoid(y @ w_glu + b_glu)


def setup_inputs(seed: int = 0) -> dict:
    key = jax.random.key(seed)
    ks = jax.random.split(key, 32)
    f32 = jnp.float32
    nrm = lambda k, shape, scale: jax.random.normal(k, shape, f32) * scale
    gain = lambda k, n: 1.0 + 0.02 * jax.random.normal(k, (n,), f32)
    n_idx = jnp.arange(SSM_STATE, dtype=f32)[None, :]
    return {
        "x": jax.random.normal(ks[0], (BATCH, SEQ, D_MODEL), f32),
        "ffn1_norm": gain(ks[1], D_MODEL),
        "ffn1_gate": nrm(ks[2], (D_MODEL, D_FF), D_MODEL ** -0.5),
        "ffn1_up": nrm(ks[3], (D_MODEL, D_FF), D_MODEL ** -0.5),
        "ffn1_down": nrm(ks[4], (D_FF, D_MODEL), D_FF ** -0.5),
        "mix_norm": gain(ks[5], D_MODEL),
        "w_in": nrm(ks[6], (D_MODEL, MIX_WIDTH), D_MODEL ** -0.5),
        "w_pool": nrm(ks[7], (POOL_GROUPS, POOL_GROUP_WIDTH, POOL_GROUP_WIDTH), POOL_GROUP_WIDTH ** -0.5),
        "pool_scale": gain(ks[8], POOL_WIDTH),
        "lam_re": -0.5 + 0.01 * jax.random.normal(ks[9], (SSM_GROUPS, SSM_STATE), f32),
        "lam_im": math.pi * n_idx + 0.01 * jax.random.normal(ks[10], (SSM_GROUPS, SSM_STATE), f32),
        "log_dt": jax.random.uniform(ks[11], (SSM_GROUPS,), f32, math.log(DT_MIN), math.log(DT_MAX)),
        "b_re": nrm(ks[12], (SSM_GROUPS, SSM_STATE, SSM_GROUP_CH), (2.0 * SSM_GROUP_CH) ** -0.5),
        "b_im": nrm(ks[13], (SSM_GROUPS, SSM_STATE, SSM_GROUP_CH), (2.0 * SSM_GROUP_CH) ** -0.5),
        "c_re": nrm(ks[14], (SSM_GROUPS, SSM_GROUP_CH, SSM_STATE), (2.0 * SSM_STATE) ** -0.5),
        "c_im": nrm(ks[15], (SSM_GROUPS, SSM_GROUP_CH, SSM_STATE), (2.0 * SSM_STATE) ** -0.5),
        "d_skip": jax.random.normal(ks[16], (SSM_WIDTH,), f32),
        "w_glu": nrm(ks[17], (SSM_WIDTH, SSM_WIDTH), SSM_WIDTH ** -0.5),
        "b_glu": nrm(ks[18], (SSM_WIDTH,), 0.01),
        "pool_out_norm": gain(ks[19], POOL_WIDTH),
        "ssm_out_norm": gain(ks[20], SSM_WIDTH),
        "w_out": nrm(ks[21], (MIX_WIDTH, D_MODEL), MIX_WIDTH ** -0.5),
        "ffn2_norm": gain(ks[22], D_MODEL),
        "ffn2_gate": nrm(ks[23], (D_MODEL, D_FF), D_MODEL ** -0.5),
        "ffn2_up": nrm(ks[24], (D_MODEL, D_FF), D_MODEL ** -0.5),
        "ffn2_down": nrm(ks[25], (D_FF, D_MODEL), D_FF ** -0.5),
        "final_norm": gain(ks[26], D_MODEL),
    }


def reference(x, ffn1_norm, ffn1_gate, ffn1_up, ffn1_down, mix_norm, w_in, w_pool, pool_scale,
              lam_re, lam_im, log_dt, b_re, b_im, c_re, c_im, d_skip, w_glu, b_glu,
              pool_out_norm, ssm_out_norm, w_out, ffn2_norm, ffn2_gate, ffn2_up, ffn2_down,
              final_norm):
    h = x
    for _ in range(DEPTH):
        h = h + 0.5 * swiglu_ffn(rms_norm(h, ffn1_norm), ffn1_gate, ffn1_up, ffn1_down)
        z = rms_norm(h, mix_norm) @ w_in
        z_pool = z[..., :POOL_WIDTH]
        z_ssm = z[..., POOL_WIDTH:]
        y_pool = causal_multiscale_pool(z_pool, w_pool, pool_scale)
        y_ssm = s5_mixer(z_ssm, lam_re, lam_im, log_dt, b_re, b_im, c_re, c_im,
                         d_skip, w_glu, b_glu)
        merged = jnp.concatenate(
            [rms_norm(y_pool, pool_out_norm), rms_norm(y_ssm, ssm_out_norm)], axis=-1)
        h = h + merged @ w_out
        h = h + 0.5 * swiglu_ffn(rms_norm(h, ffn2_norm), ffn2_gate, ffn2_up, ffn2_down)
    return rms_norm(h, final_norm)
```

```python
import math
import numpy as np
from contextlib import ExitStack
import concourse.bass as bass
import concourse.mybir as mybir
from concourse.bass_utils import run_bass_kernel_spmd

F32 = mybir.dt.float32
BF16 = mybir.dt.bfloat16
I32 = mybir.dt.int32
AF = mybir.ActivationFunctionType
ALU = mybir.AluOpType

ENGS = ["pe", "act", "dve", "pool", "sp"]
POOL_WINDOWS = (2, 4, 8, 16)
NORM_EPS = 1e-6


class Cfg:
    def __init__(self, D=4096, S=2048, T=512, NCORES=8):
        self.D, self.S, self.T, self.NCORES = D, S, T, NCORES
        self.DFF = ((8 * D // 3 + 255) // 256) * 256
        self.DC = D // 128
        self.FC = self.DFF // 128
        self.MW = D // 2
        self.MC = self.MW // 128
        self.NG = self.MW // 16
        self.NJ = self.NG // 2
        self.PGW = self.MW // 4
        self.PGC = self.PGW // 128
        self.NT = S // T
        self.TB = T // 128
        self.NCV = 4 * self.DC + 5 * self.MC


class Sched:
    def __init__(self, nc, stack):
        self.nc = nc
        self.stack = stack
        self.sem = {e: stack.enter_context(nc.semaphore("s_" + e)) for e in ENGS}
        self.cnt = {e: 0 for e in ENGS}
        self.ops = {e: [] for e in ENGS}
        self.seen = {e: {} for e in ENGS}
        self.last_w = {}
        self.readers = {}
        self.dsem = {}

    def _deps(self, reads, writes):
        toks = []
        for k in reads:
            t = self.last_w.get(k)
            if t is not None:
                toks.append(t)
        for k in writes:
            t = self.last_w.get(k)
            if t is not None:
                toks.append(t)
            toks.extend(self.readers.get(k, ()))
        return toks

    def _emit_waits(self, eng, toks):
        need = {}
        for sk, v in toks:
            if sk == "pe" and eng == "pe":
                continue
            if v > need.get(sk, 0):
                need[sk] = v
        seen = self.seen[eng]
        for sk, v in need.items():
            if seen.get(sk, 0) >= v:
                continue
            seen[sk] = v
            self.ops[eng].append(("w", sk, v))

    def _commit(self, tok, reads, writes):
        for k in reads:
            self.readers.setdefault(k, []).append(tok)
        for k in writes:
            self.last_w[k] = tok
            self.readers[k] = []

    def op(self, eng, fn, reads=(), writes=()):
        self._emit_waits(eng, self._deps(reads, writes))
        self.cnt[eng] += 1
        tok = (eng, self.cnt[eng])
        self.ops[eng].append(("o", fn))
        self._commit(tok, reads, writes)
        return tok

    def dma(self, q, slot, fn, reads=(), writes=(), n=1):
        if slot not in self.dsem:
            self.dsem[slot] = [self.stack.enter_context(self.nc.semaphore("d_" + slot)), 0]
        self._emit_waits(q, self._deps(reads, writes))
        self.dsem[slot][1] += 16 * n
        tok = ("D" + slot, self.dsem[slot][1])
        self.ops[q].append(("d", fn, slot))
        self._commit(tok, reads, writes)
        return tok

    def all_tokens(self):
        toks = [(e, self.cnt[e]) for e in ENGS if self.cnt[e] > 0]
        toks += [("D" + s, v[1]) for s, v in self.dsem.items() if v[1] > 0]
        return toks

    def barrier(self):
        toks = self.all_tokens()
        for e in ENGS:
            self._emit_waits(e, toks)
        self.last_w.clear()
        self.readers.clear()

    def _semh(self, sk):
        if sk.startswith("D"):
            return self.dsem[sk[1:]][0]
        return self.sem[sk]

    def replay(self, eng, handle):
        sem = self.sem[eng]
        for o in self.ops[eng]:
            if o[0] == "w":
                handle.wait_ge(self._semh(o[1]), o[2])
            elif o[0] == "o":
                o[1](handle).then_inc(sem, 1)
            else:
                o[1](handle, self.dsem[o[2]][0])

    def run_block(self):
        with self.nc.Block() as block:
            @block.tensor
            def _(e):
                self.replay("pe", e)

            @block.scalar
            def _(e):
                self.replay("act", e)

            @block.vector
            def _(e):
                self.replay("dve", e)

            @block.gpsimd
            def _(e):
                self.replay("pool", e)

            @block.sync
            def _(e):
                self.replay("sp", e)


def build_program(cfg):
    D, DFF, S_, T = cfg.D, cfg.DFF, cfg.S, cfg.T
    DC, FC, MW, MC, NG, NJ, PGC, NT, TB = cfg.DC, cfg.FC, cfg.MW, cfg.MC, cfg.NG, cfg.NJ, cfg.PGC, cfg.NT, cfg.TB
    KH = DC // 2
    nc = bass.Bass("TRN2", target_bir_lowering=False)

    def din(name, shape, dt=F32):
        return nc.dram_tensor(name, list(shape), dt, kind="ExternalInput").ap()

    x_d = din("x", [S_, D])
    w = {}
    for f in ("ffn1", "ffn2"):
        w[f + "_gate"] = din(f + "_gate", [D, DFF])
        w[f + "_up"] = din(f + "_up", [D, DFF])
        w[f + "_down"] = din(f + "_down", [DFF, D])
    w_in_d = din("w_in", [D, D])
    w_out_d = din("w_out", [D, D])
    w_glu_d = din("w_glu", [MW, MW])
    w_pool_d = din("w_pool", [4 * cfg.PGW, cfg.PGW])
    cvec_d = din("cvec", [128, cfg.NCV])
    s5p_d = din("s5p", [128, 3, NJ])
    bz_d = din("bz", [128, NJ, 2, 128])
    cz_d = din("cz", [128, NJ, 2, 128])
    cmat_d = din("cmat", [128, 2, 128])
    iota_d = din("iota", [128, T])
    icnt_d = din("icnt", [128, 4, 16])
    out_d = nc.dram_tensor("out", [S_, D], F32, kind="ExternalOutput").ap()
    tab_d = nc.dram_tensor("tab_scr", [NJ, 128, 2, T], BF16).ap()
    s5w_d = nc.dram_tensor("s5w_scr", [MC, 128, 16, 128], BF16).ap()

    CV_F1, CV_MIX, CV_F2, CV_FIN = 0, DC, 2 * DC, 3 * DC
    CV_PSC = 4 * DC
    CV_PG, CV_SG, CV_DSK, CV_BGL = CV_PSC + MC, CV_PSC + 2 * MC, CV_PSC + 3 * MC, CV_PSC + 4 * MC

    with ExitStack() as st:
        S = Sched(nc, st)
        sb_state = {"off": (nc.sbuf_base + 63) // 64 * 64, "n": 0}

        def alloc(shape, dt, at=None):
            esz = 4 if dt in (F32, I32) else 2
            nbytes = int(np.prod(shape[1:])) * esz
            nbytes = (nbytes + 63) // 64 * 64
            if at is None:
                off = sb_state["off"]
                sb_state["off"] += nbytes
                assert sb_state["off"] <= nc.sbuf_top, ("SBUF overflow", sb_state["off"], nc.sbuf_top)
            else:
                off = at
            sb_state["n"] += 1
            sb_state["last"] = off
            return nc.alloc_sbuf_tensor_at("t%d" % sb_state["n"], list(shape), dt, offset=off)

        h = alloc([128, DC, T], F32)
        xn = alloc([128, DC, T], BF16)
        ACTC = max(16, MC)
        act = alloc([128, ACTC, T], BF16)
        ubf = alloc([128, max(16, MC), T], BF16)
        ubf_off = sb_state["last"]
        cv = alloc([128, cfg.NCV], F32)
        cmat = alloc([128, 2, 128], F32)
        ident = cmat[:, 0, :]
        ones = cmat[:, 1, :]
        icnt = alloc([128, 4, 16], F32)
        s5p = alloc([128, 3, NJ], F32)
        s5v = alloc([128, 14, NJ], F32)
        V_DT, V_R, V_TH, V_GRE, V_GIM, V_NGIM, V_XLR, V_XLI, V_T0, V_T1, V_T2, V_T3, V_CL, V_SL = range(14)
        halo = alloc([128, MC, 16], F32)
        kcol = alloc([128, 2], F32)
        rstdA = alloc([128, T], F32)
        rstdP = alloc([128, T], F32)
        rstdS = alloc([128, T], F32)
        NTMP = 6
        tmp = []
        tmp_off = []
        for _ in range(NTMP):
            tmp.append(alloc([128, T], F32))
            tmp_off.append(sb_state["last"])
        zt = alloc([128, 16 + T], F32)
        zs = [alloc([128, 16 + T], F32) for _ in range(2)]
        NS, NB = 3, 6
        arena0 = sb_state["off"]
        stage = [alloc([128, 2048], F32) for _ in range(NS)]
        wbf = [alloc([128, 2048], BF16) for _ in range(NB)]
        arena_end = sb_state["off"]
        stage = stage + [alloc([128, 2048], F32, at=ubf_off), alloc([128, 2048], F32, at=ubf_off + 8192)]
        ns_active = {"n": NS}
        ARENA = arena_end - arena0
        ao = {"o": arena0}

        def aalloc(shape, dt):
            esz = 4 if dt in (F32, I32) else 2
            nbytes = (int(np.prod(shape[1:])) * esz + 63) // 64 * 64
            t = alloc(shape, dt, at=ao["o"])
            ao["o"] += nbytes
            assert ao["o"] <= arena_end, "arena overflow"
            return t

        cs = [aalloc([128, 2, T], BF16) for _ in range(3)]
        bb = [[aalloc([128, T], BF16) for _ in range(2)] for _ in range(2)]
        pp = [[aalloc([128, T], BF16) for _ in range(4)] for _ in range(2)]
        identb = aalloc([128, 128], BF16)
        nidentb = aalloc([128, 128], BF16)
        v32 = [[aalloc([128, T], F32) for _ in range(2)] for _ in range(2)]
        v16 = [[aalloc([128, T], BF16) for _ in range(2)] for _ in range(2)]
        xri = [[aalloc([128, T], BF16) for _ in range(2)] for _ in range(2)]
        s5w = [aalloc([128, 16, 128], BF16) for _ in range(2)]
        ao["o"] = arena0
        iost = [aalloc([128, D], F32) for _ in range(2)] if 2 * D * 4 <= ARENA else [aalloc([128, D], F32)]
        bzc = alloc([128, 4, 2, 128], F32, at=ubf_off)
        czc = alloc([128, 4, 2, 128], F32, at=ubf_off + 4096)
        s5o = alloc([128, 4, 4, 128], BF16, at=ubf_off + 8192)
        tab_ta = alloc([128, 128], F32, at=ubf_off + 12288)
        tab_tb = alloc([128, 128], F32, at=ubf_off + 12288 + 512)
        ki0 = alloc([128, T], I32, at=tmp_off[2])
        rbuf = [zt, rstdP]
        csb = alloc([128, 2, T], BF16, at=ubf_off + 13312)
        assert 13312 + 4 * T <= 2 * max(16, MC) * T
        print('SBUF used', sb_state['off'], 'of', nc.sbuf_top, 'arena', ARENA, flush=True)
        ps = [st.enter_context(nc.psum_tensor("ps%d" % i, [128, 512], F32)) for i in range(8)]

        def PS(b):
            return ps[b][:, 0:T]

        rr = {"cast": 0, "st": 0, "wb": 0, "ev": 0}
        CAST_ENGS = ["act", "dve", "act", "dve", "act"]

        def copy_on(eng, out, in_):
            if eng == "act":
                return lambda e: e.copy(out=out, in_=in_)
            return lambda e: e.tensor_copy(out=out, in_=in_)

        def wload(W, r0, kc, c0, ncol, ceng=None):
            assert kc * ncol <= 2048
            s = rr["st"] % ns_active["n"]
            rr["st"] += 1
            b = rr["wb"] % NB
            rr["wb"] += 1
            src = W[r0:r0 + kc * 128, c0:c0 + ncol].rearrange("(k p) n -> p k n", p=128)
            dst = stage[s][:, 0:kc * ncol].rearrange("p (k n) -> p k n", n=ncol)
            S.dma("sp", "st%d" % s, lambda q, sem: q.dma_start(out=dst, in_=src).then_inc(sem, 16),
                  writes=[("st", s)])
            if ceng is None:
                eng = CAST_ENGS[rr["cast"] % len(CAST_ENGS)]
                rr["cast"] += 1
            else:
                eng = ceng
            S.op(eng, copy_on(eng, wbf[b][:, 0:kc * ncol], stage[s][:, 0:kc * ncol]),
                 reads=[("st", s)], writes=[("wb", b)])
            return (wbf[b][:, 0:kc * ncol].rearrange("p (k n) -> p k n", n=ncol), ("wb", b))

        def run_steps(steps, PF=2, bgl=None):
            n = len(steps)
            nsl = [(st_[2] if len(st_) > 2 else 2) for st_ in steps]
            stride = max(1, (n - 8) // max(1, len(bgl))) if bgl else 0
            loaded = {}
            nxt = 0
            inflight = 0
            def prefetch():
                nonlocal nxt, inflight
                while nxt < n and (inflight + nsl[nxt] <= NB):
                    loaded[nxt] = steps[nxt][0]()
                    inflight += nsl[nxt]
                    nxt += 1
            while nxt < min(n, 2):
                loaded[nxt] = steps[nxt][0]()
                inflight += nsl[nxt]
                nxt += 1
            for k in range(n):
                assert k in loaded
                steps[k][1](loaded.pop(k))
                inflight -= nsl[k]
                prefetch()
                if bgl and k % stride == 0:
                    bgl.pop(0)()
            while bgl:
                bgl.pop(0)()

        def load_col_tile(W, r0, nk, c0):
            tiles = []
            k = 0
            while k < nk:
                kc = min(16, nk - k)
                v, key = wload(W, r0 + k * 128, kc, c0, 128)
                tiles.append((v, key, kc))
                k += kc
            return tiles

        def mm_col_tile(tiles, rhs_fn, rhs_keys, bank, first=True, last=True):
            def f(e):
                r = None
                kk = 0
                nk = sum(t[2] for t in tiles)
                for (v, key, kc) in tiles:
                    for k in range(kc):
                        r = e.matmul(PS(bank), lhsT=v[:, k, :], rhs=rhs_fn(kk),
                                     start=(first and kk == 0), stop=(last and kk == nk - 1))
                        kk += 1
                return r
            S.op("pe", f, reads=[t[1] for t in tiles] + list(rhs_keys), writes=[("ps", bank)])

        def make_rstd(bank, N, dst, dkey):
            S.op("dve", lambda e: e.tensor_scalar(out=dst[:], in0=PS(bank), scalar1=1.0 / N, scalar2=NORM_EPS,
                                                  op0=ALU.mult, op1=ALU.add),
                 reads=[("ps", bank)], writes=[dkey])
            S.op("act", lambda e: e.activation(out=dst[:], in_=dst[:], func=AF.Sqrt), reads=[dkey], writes=[dkey])
            S.op("dve", lambda e: e.reciprocal(out=dst[:], in_=dst[:]), reads=[dkey], writes=[dkey])

        sqi = {"i": 0}

        def stats_add(src_ap, src_keys, bank, first, last, scale=None):
            i = sqi["i"] % 2
            sqi["i"] += 1
            sq = tmp[4 + i]
            if scale is None:
                S.op("act", lambda e: e.activation(out=sq[:], in_=src_ap, func=AF.Square),
                     reads=src_keys, writes=[("tmp", 4 + i)])
            else:
                S.op("act", lambda e: e.activation(out=sq[:], in_=src_ap, func=AF.Square, scale=scale),
                     reads=src_keys, writes=[("tmp", 4 + i)])
            S.op("pe", lambda e: e.matmul(PS(bank), lhsT=ones, rhs=sq[:], start=first, stop=last),
                 reads=[("tmp", 4 + i)], writes=[("ps", bank)])

        def norm_stats_h():
            for dc in range(DC):
                if dc % 2 == 0:
                    stats_add(h[:, dc, :], [("h", dc)], 7, dc == 0, dc == DC - 1)
                else:
                    i = 2 + (dc // 2) % 2
                    sq = tmp[i]
                    S.op("dve", lambda e, dc=dc, sq=sq: e.tensor_tensor(out=sq[:], in0=h[:, dc, :], in1=h[:, dc, :], op=ALU.mult),
                         reads=[("h", dc)], writes=[("tmp", i)])
                    S.op("pe", lambda e, dc=dc, sq=sq: e.matmul(PS(7), lhsT=ones, rhs=sq[:], start=(dc == 0), stop=(dc == DC - 1)),
                         reads=[("tmp", i)], writes=[("ps", 7)])

        def norm_h(gcol):
            sqb = [2, 4, 5]
            for dc in range(DC):
                i = sqb[dc % 3]
                sq = tmp[i]
                S.op("dve", lambda e, dc=dc, sq=sq: e.tensor_tensor(out=sq[:], in0=h[:, dc, :], in1=h[:, dc, :], op=ALU.mult),
                     reads=[("h", dc)], writes=[("tmp", i)])
                S.op("pe", lambda e, dc=dc, sq=sq: e.matmul(PS(7), lhsT=ones, rhs=sq[:], start=(dc == 0), stop=(dc == DC - 1)),
                     reads=[("tmp", i)], writes=[("ps", 7)])
            for dc in range(DC):
                S.op("act", lambda e, dc=dc: e.activation(out=xn[:, dc, :], in_=h[:, dc, :], func=AF.Identity,
                                                          scale=cv[:, gcol + dc:gcol + dc + 1]),
                     reads=[("h", dc)], writes=[("xn", dc)])
            make_rstd(7, D, rstdA, "rstdA")

        def load_x(ti):
            for tb in range(TB):
                sl = tb % len(iost)
                xin = iost[sl]
                r0 = ti * T + tb * 128
                S.dma("sp", "io%d" % sl, lambda q, sem, xin=xin, r0=r0: q.dma_start(out=xin[:], in_=x_d[r0:r0 + 128, :]).then_inc(sem, 16),
                      writes=[("io", sl)])
                for q4 in range(DC // 4):
                    b = q4 % 4
                    def f(e, q4=q4, b=b, xin=xin):
                        r = None
                        for i in range(4):
                            r = e.transpose(out=ps[b][:, i * 128:(i + 1) * 128],
                                            in_=xin[:, (4 * q4 + i) * 128:(4 * q4 + i + 1) * 128], identity=ident)
                        return r
                    S.op("pe", f, reads=[("io", sl), "c1"], writes=[("ps", b)])
                    eng = "act" if q4 % 2 == 0 else "dve"
                    S.op(eng, copy_on(eng, h[:, 4 * q4:4 * q4 + 4, tb * 128:(tb + 1) * 128],
                                      ps[b][:, 0:512].rearrange("p (a n) -> p a n", n=128)),
                         reads=[("ps", b)], writes=[("h", 4 * q4 + i) for i in range(4)])

        def ffn(wg, wu, wd, bgl=None):
            groups = [list(range(g0, min(g0 + 8, FC))) for g0 in range(0, FC, 8)]
            gu_steps, d_steps = [], []
            cnt = {"gu": 0, "d": 0}
            for gi, grp in enumerate(groups):
                ab = (gi % 2) * 8
                gu_steps.append([])
                d_steps.append([])
                for li, fc in enumerate(grp):
                    gb = cnt["gu"] % 2
                    ub = 2 + cnt["gu"] % 2
                    cnt["gu"] += 1
                    xkeys = [("xn", k) for k in range(DC)]

                    def cg(tl, gb=gb):
                        mm_col_tile(tl, lambda k: xn[:, k, :], xkeys, gb)

                    def cu(tl, gb=gb, ub=ub, ai=ab + li):
                        mm_col_tile(tl, lambda k: xn[:, k, :], xkeys, ub)
                        ti_ = gb
                        tu_ = 4 + gb
                        S.op("dve", lambda e: e.tensor_tensor(out=tmp[ti_][:], in0=PS(gb), in1=rstdA[:], op=ALU.mult),
                             reads=[("ps", gb), "rstdA"], writes=[("tmp", ti_)])
                        S.op("act", lambda e: e.activation(out=tmp[ti_][:], in_=tmp[ti_][:], func=AF.Silu),
                             reads=[("tmp", ti_)], writes=[("tmp", ti_)])
                        S.op("dve", lambda e: e.tensor_tensor(out=tmp[tu_][:], in0=PS(ub), in1=rstdA[:], op=ALU.mult),
                             reads=[("ps", ub), "rstdA"], writes=[("tmp", tu_)])
                        S.op("dve", lambda e: e.tensor_tensor(out=act[:, ai, :], in0=tmp[ti_][:], in1=tmp[tu_][:], op=ALU.mult),
                             reads=[("tmp", ti_), ("tmp", tu_)], writes=[("act", ai)])
                    gu_steps[gi].append((lambda fc=fc: load_col_tile(wg, 0, DC, fc * 128), cg, (DC + 15) // 16))
                    gu_steps[gi].append((lambda fc=fc: load_col_tile(wu, 0, DC, fc * 128), cu, (DC + 15) // 16))
                ng = len(grp)
                for dp in range(DC // 2):
                    def ld(grp=grp, ng=ng, dp=dp):
                        return wload(wd, grp[0] * 128, ng, dp * 256, 256)

                    def cd(tl, ng=ng, dp=dp, ab=ab):
                        v, key = tl
                        for i in range(2):
                            b = 4 + cnt["d"] % 4
                            cnt["d"] += 1
                            dc = 2 * dp + i

                            def f(e, b=b, i=i):
                                r = None
                                for k in range(ng):
                                    r = e.matmul(PS(b), lhsT=v[:, k, i * 128:(i + 1) * 128], rhs=act[:, ab + k, :],
                                                 start=(k == 0), stop=(k == ng - 1))
                                return r
                            S.op("pe", f, reads=[key] + [("act", ab + k) for k in range(ng)], writes=[("ps", b)])
                            S.op("dve", lambda e, b=b, dc=dc: e.scalar_tensor_tensor(
                                out=h[:, dc, :], in0=PS(b), scalar=0.5, in1=h[:, dc, :], op0=ALU.mult, op1=ALU.add),
                                reads=[("ps", b), ("h", dc)], writes=[("h", dc)])
                    d_steps[gi].append((ld, cd, 1))
            steps = list(gu_steps[0])
            for gi in range(len(groups)):
                if gi + 1 < len(groups):
                    steps += gu_steps[gi + 1]
                steps += d_steps[gi]
            run_steps(steps, PF=2, bgl=bgl)

        def range_reduce(ph, phk, kint, kf, kfk, r, rk):
            C1 = 6.28125
            C2 = 2.0 * math.pi - C1
            S.op("dve", lambda e: e.tensor_scalar(out=kf, in0=ph, scalar1=1.0 / (2.0 * math.pi), scalar2=None, op0=ALU.mult),
                 reads=[phk], writes=[kfk])
            S.op("dve", lambda e: e.tensor_copy(out=kint, in_=kf), reads=[kfk], writes=[("tmp", 2)])
            S.op("dve", lambda e: e.tensor_copy(out=kf, in_=kint), reads=[("tmp", 2)], writes=[kfk])
            S.op("dve", lambda e: e.scalar_tensor_tensor(out=r, in0=kf, scalar=-C1, in1=ph, op0=ALU.mult, op1=ALU.add),
                 reads=[kfk, phk], writes=[rk])
            S.op("dve", lambda e: e.scalar_tensor_tensor(out=r, in0=kf, scalar=-C2, in1=r, op0=ALU.mult, op1=ALU.add),
                 reads=[kfk, rk], writes=[rk])
            S.op("dve", lambda e: e.tensor_scalar(out=r, in0=r, scalar1=-3.1415925, scalar2=3.1415925, op0=ALU.max, op1=ALU.min),
                 reads=[rk], writes=[rk])

        def sincos(r, rk, na, nak, out_sin, sk, out_cos, ck):
            S.op("act", lambda e: e.activation(out=na, in_=r, func=AF.Abs), reads=[rk], writes=[nak])
            S.op("act", lambda e: e.activation(out=out_sin, in_=r, func=AF.Sin), reads=[rk], writes=[sk])
            S.op("act", lambda e: e.activation(out=out_cos, in_=na, func=AF.Sin, bias=kcol[:, 0:1], scale=-1.0),
                 reads=[nak, "kcol"], writes=[ck])

        bg = []

        def setup():
            for (dst, src, nm) in ((cv, cvec_d, "c0"), (cmat, cmat_d, "c1"), (icnt, icnt_d, "c2"), (s5p, s5p_d, "c3")):
                S.dma("sp", nm, lambda q, sem, dst=dst, src=src: q.dma_start(out=dst[:], in_=src).then_inc(sem, 16),
                      writes=[nm])
            iota = rstdS
            S.dma("sp", "c4", lambda q, sem: q.dma_start(out=iota[:], in_=iota_d).then_inc(sem, 16), writes=["iota"])
            S.op("dve", lambda e: e.memset(halo[:], 0.0), writes=["halo"])
            S.op("dve", lambda e: e.memset(s5v[:], 0.0), writes=["s5v"])
            S.op("dve", lambda e: e.memset(kcol[:, 0:1], math.pi / 2.0), writes=["kcol"])
            lre, lim, ldt = s5p[:, 0, :], s5p[:, 1, :], s5p[:, 2, :]
            V = lambda i: s5v[:, i, :]
            K = "s5v"
            S.op("act", lambda e: e.activation(out=V(V_DT), in_=ldt, func=AF.Exp), reads=["c3", K], writes=[K])
            S.op("dve", lambda e: e.tensor_tensor(out=V(V_T0), in0=lre, in1=V(V_DT), op=ALU.mult), reads=[K], writes=[K])
            S.op("act", lambda e: e.activation(out=V(V_R), in_=V(V_T0), func=AF.Exp), reads=[K], writes=[K])
            S.op("dve", lambda e: e.tensor_tensor(out=V(V_TH), in0=lim, in1=V(V_DT), op=ALU.mult), reads=[K], writes=[K])
            kint_s = ki0[:, 0:NJ]
            range_reduce(V(V_TH), K, kint_s, V(V_T0), K, V(V_T3), K)
            sincos(V(V_T3), K, V(V_T0), K, V(V_T1), K, V(V_T2), K)
            S.op("dve", lambda e: e.tensor_tensor(out=V(V_T2), in0=V(V_T2), in1=V(V_R), op=ALU.mult), reads=[K], writes=[K])
            S.op("dve", lambda e: e.tensor_tensor(out=V(V_T1), in0=V(V_T1), in1=V(V_R), op=ALU.mult), reads=[K], writes=[K])
            S.op("dve", lambda e: e.tensor_scalar(out=V(V_T2), in0=V(V_T2), scalar1=-1.0, scalar2=None, op0=ALU.add), reads=[K], writes=[K])
            S.op("dve", lambda e: e.tensor_tensor(out=V(V_T0), in0=lre, in1=lre, op=ALU.mult), reads=[K], writes=[K])
            S.op("dve", lambda e: e.tensor_tensor(out=V(V_T3), in0=lim, in1=lim, op=ALU.mult), reads=[K], writes=[K])
            S.op("dve", lambda e: e.tensor_tensor(out=V(V_T0), in0=V(V_T0), in1=V(V_T3), op=ALU.add), reads=[K], writes=[K])
            S.op("dve", lambda e: e.reciprocal(out=V(V_T0), in_=V(V_T0)), reads=[K], writes=[K])
            S.op("dve", lambda e: e.tensor_tensor(out=V(V_GRE), in0=V(V_T2), in1=lre, op=ALU.mult), reads=[K], writes=[K])
            S.op("dve", lambda e: e.tensor_tensor(out=V(V_T3), in0=V(V_T1), in1=lim, op=ALU.mult), reads=[K], writes=[K])
            S.op("dve", lambda e: e.tensor_tensor(out=V(V_GRE), in0=V(V_GRE), in1=V(V_T3), op=ALU.add), reads=[K], writes=[K])
            S.op("dve", lambda e: e.tensor_tensor(out=V(V_GRE), in0=V(V_GRE), in1=V(V_T0), op=ALU.mult), reads=[K], writes=[K])
            S.op("dve", lambda e: e.tensor_tensor(out=V(V_GIM), in0=V(V_T1), in1=lre, op=ALU.mult), reads=[K], writes=[K])
            S.op("dve", lambda e: e.tensor_tensor(out=V(V_T3), in0=V(V_T2), in1=lim, op=ALU.mult), reads=[K], writes=[K])
            S.op("dve", lambda e: e.tensor_tensor(out=V(V_GIM), in0=V(V_GIM), in1=V(V_T3), op=ALU.subtract), reads=[K], writes=[K])
            S.op("dve", lambda e: e.tensor_tensor(out=V(V_GIM), in0=V(V_GIM), in1=V(V_T0), op=ALU.mult), reads=[K], writes=[K])
            S.op("dve", lambda e: e.tensor_scalar(out=V(V_NGIM), in0=V(V_GIM), scalar1=-1.0, scalar2=None, op0=ALU.mult), reads=[K], writes=[K])
            S.op("dve", lambda e: e.memset(s5v[:, V_XLR:V_XLI + 1, :], 0.0), reads=[K], writes=[K])
            tabA, tabB, tabC = [], [], []
            for j in range(NJ):
                r_ = rbuf[j % 2][:, 0:T]
                rk = ("rr", j % 2)

                def tA(j=j, r_=r_, rk=rk):
                    ph, kf = zs[0][:, 0:T], zs[1][:, 0:T]
                    S.op("dve", lambda e: e.tensor_scalar(out=ph, in0=iota[:], scalar1=s5v[:, V_TH, j:j + 1], scalar2=None, op0=ALU.mult),
                         reads=["iota", K], writes=["ph"])
                    range_reduce(ph, "ph", ki0[:], kf, "kf", r_, rk)

                def tB(j=j, r_=r_, rk=rk):
                    na = tmp[3]
                    S.op("act", lambda e: e.activation(out=na[:], in_=r_, func=AF.Abs), reads=[rk], writes=[("tmp", 3)])
                    S.op("act", lambda e: e.activation(out=csb[:, 1, :], in_=r_, func=AF.Sin), reads=[rk], writes=["csb"])
                    S.op("act", lambda e: e.activation(out=s5v[:, V_SL, j:j + 1], in_=r_[:, T - 1:T], func=AF.Sin), reads=[rk], writes=[("csl", j)])
                    S.op("act", lambda e: e.activation(out=csb[:, 0, :], in_=na[:], func=AF.Sin, bias=kcol[:, 0:1], scale=-1.0),
                         reads=[("tmp", 3), "kcol"], writes=["csb"])
                    S.op("act", lambda e: e.activation(out=s5v[:, V_CL, j:j + 1], in_=na[:, T - 1:T], func=AF.Sin, bias=kcol[:, 0:1], scale=-1.0),
                         reads=[("tmp", 3), "kcol"], writes=[("csl", j)])

                def tC(j=j):
                    S.dma("sp", "tabc", lambda q, sem: q.dma_start(out=tab_d[j], in_=csb[:]).then_inc(sem, 16),
                          reads=["csb"], writes=[("tab", j)])
                tabA.append(tA)
                tabB.append(tB)
                tabC.append(tC)
            for j in range(NJ + 1):
                def slot_even(j=j):
                    if j >= 1:
                        tabC[j - 1]()
                    if j < NJ:
                        tabA[j]()
                bg.append(slot_even)
                if j < NJ:
                    bg.append(tabB[j])
            m1, m2, m3 = [], [], []
            for c in range(MC):
                def mt1(c=c):
                    S.dma("sp", "bzc", lambda q, sem: q.dma_start(out=bzc[:], in_=bz_d[:, 4 * c:4 * c + 4, :, :]).then_inc(sem, 16), writes=["bzc"])
                    S.dma("sp", "czc", lambda q, sem: q.dma_start(out=czc[:], in_=cz_d[:, 4 * c:4 * c + 4, :, :]).then_inc(sem, 16), writes=["czc"])

                def mt2(c=c):
                    S.op("act", lambda e: e.copy(out=s5o[:, :, 0:2, :], in_=bzc[:]), reads=["bzc"], writes=["s5o"])
                    for jj in range(4):
                        j = 4 * c + jj
                        czr, czi = czc[:, jj, 0, :], czc[:, jj, 1, :]
                        ta, tb_ = tab_ta[:], tab_tb[:]
                        S.op("dve", lambda e, j=j, czi=czi: e.tensor_scalar(out=ta, in0=czi, scalar1=s5v[:, V_GIM, j:j + 1], scalar2=None, op0=ALU.mult),
                             reads=["czc", K], writes=["ta"])
                        S.op("dve", lambda e, j=j, czr=czr, jj=jj: e.scalar_tensor_tensor(out=s5o[:, jj, 2, :], in0=czr, scalar=s5v[:, V_GRE, j:j + 1], in1=ta,
                                                                                         op0=ALU.mult, op1=ALU.subtract),
                             reads=["czc", K, "ta"], writes=["s5o"])
                        S.op("dve", lambda e, j=j, czi=czi: e.tensor_scalar(out=tb_, in0=czi, scalar1=s5v[:, V_GRE, j:j + 1], scalar2=None, op0=ALU.mult),
                             reads=["czc", K], writes=["tb"])
                        S.op("dve", lambda e, j=j, czr=czr, jj=jj: e.scalar_tensor_tensor(out=s5o[:, jj, 3, :], in0=czr, scalar=s5v[:, V_NGIM, j:j + 1], in1=tb_,
                                                                                         op0=ALU.mult, op1=ALU.subtract),
                             reads=["czc", K, "tb"], writes=["s5o"])

                def mt3(c=c):
                    S.dma("sp", "s5ow", lambda q, sem: q.dma_start(out=s5w_d[c], in_=s5o[:].rearrange("p a b n -> p (a b) n")).then_inc(sem, 16),
                          reads=["s5o"], writes=[("s5wd", c)])
                m1.append(mt1)
                m2.append(mt2)
                m3.append(mt3)
            for c in range(MC + 1):
                def mslot(c=c):
                    if c >= 1:
                        m3[c - 1]()
                    if c < MC:
                        m1[c]()
                bg.append(mslot)
                if c < MC:
                    bg.append(m2[c])

        def mixer(ti):
            norm_h(CV_MIX)
            xkeys = [("xn", k) for k in range(DC)]
            steps = []
            for oc in range(2 * MC):
                def cin(tl, oc=oc):
                    b = oc % 4
                    mm_col_tile(tl, lambda k: xn[:, k, :], xkeys, b)
                    if oc >= MC:
                        c = oc - MC
                        S.op("dve", lambda e: e.tensor_tensor(out=ubf[:, c, :], in0=PS(b), in1=rstdA[:], op=ALU.mult),
                             reads=[("ps", b), "rstdA"], writes=[("ubf", c)])
                        return
                    c = oc
                    wi = c // PGC
                    win = POOL_WINDOWS[wi]
                    S.op("pool", lambda e: e.tensor_copy(out=zt[:, 0:16], in_=halo[:, c, :]), reads=[("halo", c)], writes=["zt"])
                    S.op("dve", lambda e: e.tensor_tensor(out=zt[:, 16:16 + T], in0=PS(b), in1=rstdA[:], op=ALU.mult),
                         reads=[("ps", b), "rstdA"], writes=["zt"])
                    S.op("pool", lambda e: e.tensor_copy(out=halo[:, c, :], in_=zt[:, T:T + 16]), reads=["zt"], writes=[("halo", c)])
                    src, sk = zt, "zt"
                    lag = 1
                    k = 0
                    while lag < win:
                        dst = zs[k % 2]
                        dk = ("zs", k % 2)
                        lo = 2 * lag - 1
                        S.op("pool", lambda e, src=src, dst=dst, lo=lo, lag=lag: e.tensor_tensor(
                            out=dst[:, lo:16 + T], in0=src[:, lo:16 + T], in1=src[:, lo - lag:16 + T - lag], op=ALU.add),
                            reads=[sk], writes=[dk])
                        src, sk = dst, dk
                        lag *= 2
                        k += 1
                    S.op("dve", lambda e, src=src: e.scalar_tensor_tensor(
                        out=act[:, c, :], in0=src[:, 16:16 + T], scalar=1.0 / win, in1=zt[:, 16:16 + T],
                        op0=ALU.mult, op1=ALU.subtract), reads=[sk, "zt"], writes=[("act", c)])
                    if ti == 0:
                        S.op("dve", lambda e, src=src: e.tensor_tensor(out=tmp[2][:, 0:16], in0=src[:, 16:32], in1=icnt[:, wi, :], op=ALU.mult),
                             reads=[sk], writes=[("tmp", 2)])
                        S.op("dve", lambda e: e.tensor_tensor(out=act[:, c, 0:16], in0=tmp[2][:, 0:16], in1=zt[:, 16:32], op=ALU.subtract),
                             reads=[("tmp", 2), "zt"], writes=[("act", c)])
                steps.append((lambda oc=oc: load_col_tile(w_in_d, 0, DC, oc * 128), cin, (DC + 15) // 16))
            run_steps(steps, PF=2)
            steps = []
            for g in range(4):
                for o in range(PGC):
                    c = g * PGC + o

                    def cp(tl, g=g, c=c):
                        b = 4 + c % 2
                        mm_col_tile(tl, lambda k: act[:, g * PGC + k, :], [("act", g * PGC + k) for k in range(PGC)], b)
                        t = tmp[c % 2]
                        S.op("act", lambda e: e.activation(out=t[:], in_=PS(b), func=AF.Identity, scale=cv[:, CV_PSC + c:CV_PSC + c + 1]),
                             reads=[("ps", b)], writes=[("tmp", c % 2)])
                        stats_add(t[:], [("tmp", c % 2)], 7, c == 0, c == MC - 1)
                        S.op("dve", lambda e: e.tensor_scalar(out=xn[:, c, :], in0=t[:], scalar1=cv[:, CV_PG + c:CV_PG + c + 1], scalar2=None, op0=ALU.mult),
                             reads=[("tmp", c % 2)], writes=[("xn", c)])
                    steps.append((lambda g=g, o=o: load_col_tile(w_pool_d, g * cfg.PGW, PGC, o * 128), cp, 1))
            run_steps(steps, PF=2)
            make_rstd(7, MW, rstdP, "rstdP")
            S.barrier()
            GK = 2.0 * math.sqrt(2.0 / math.pi)

            def s5w_load(c):
                wb_ = s5w[c % 2]
                S.dma("sp", "s5w%d" % (c % 2), lambda q, sem: q.dma_start(out=wb_[:], in_=s5w_d[c]).then_inc(sem, 16),
                      reads=[("s5wd", c)], writes=[("s5w", c % 2)])

            def tab_load(j):
                ct = cs[j % 3]
                S.dma("sp", "cs%d" % (j % 3), lambda q, sem: q.dma_start(out=ct[:], in_=tab_d[j]).then_inc(sem, 16),
                      reads=[("tab", j)], writes=[("cs", j % 3)])

            def stage_pe(j):
                c, jj, q = j // 4, j % 4, j % 2
                wb_ = s5w[c % 2]
                S.op("pe", lambda e: e.matmul(PS(0), lhsT=wb_[:, jj * 4 + 0, :], rhs=ubf[:, c, :], start=True, stop=True),
                     reads=[("s5w", c % 2), ("ubf", c)], writes=[("ps", 0)])
                S.op("pe", lambda e: e.matmul(PS(1), lhsT=wb_[:, jj * 4 + 1, :], rhs=ubf[:, c, :], start=True, stop=True),
                     reads=[("s5w", c % 2), ("ubf", c)], writes=[("ps", 1)])
                b0, b1 = bb[q]
                S.op("act", lambda e: e.copy(out=b0[:], in_=PS(0)), reads=[("ps", 0)], writes=[("bb", q, 0)])
                S.op("act", lambda e: e.copy(out=b1[:], in_=PS(1)), reads=[("ps", 1)], writes=[("bb", q, 1)])

            def pe_pair(bank_a, bank_b, ps_, signs, kp, extra=()):
                def f(e):
                    e.matmul(PS(bank_a), lhsT=identb[:], rhs=ps_[0][:], start=True, stop=False)
                    e.matmul(PS(bank_a), lhsT=(identb if signs[0] > 0 else nidentb)[:], rhs=ps_[1][:], start=False, stop=True)
                    e.matmul(PS(bank_b), lhsT=identb[:], rhs=ps_[2][:], start=True, stop=False)
                    return e.matmul(PS(bank_b), lhsT=(identb if signs[1] > 0 else nidentb)[:], rhs=ps_[3][:], start=False, stop=True)
                S.op("pe", f, reads=list(kp) + ["identb", "nidentb"] + list(extra), writes=[("ps", bank_a), ("ps", bank_b)])

            def stage_f(j):
                q = j % 2
                if j % 4 == 1 and j // 4 + 1 < MC:
                    s5w_load(j // 4 + 1)
                if j + 1 < NJ:
                    tab_load(j + 1)
                ct, ck = cs[j % 3], ("cs", j % 3)
                cos_, sin_ = ct[:, 0, :], ct[:, 1, :]
                b0, b1 = bb[q]
                p0, p1, p2, p3 = pp[q]
                kb = [("bb", q, 0), ("bb", q, 1)]
                kp = [("pp", q, i) for i in range(4)]
                S.op("dve", lambda e: e.tensor_tensor(out=p0[:], in0=b0[:], in1=cos_, op=ALU.mult), reads=[kb[0], ck], writes=[kp[0]])
                S.op("dve", lambda e: e.tensor_tensor(out=p1[:], in0=b1[:], in1=sin_, op=ALU.mult), reads=[kb[1], ck], writes=[kp[1]])
                S.op("dve", lambda e: e.tensor_tensor(out=p2[:], in0=b1[:], in1=cos_, op=ALU.mult), reads=[kb[1], ck], writes=[kp[2]])
                S.op("dve", lambda e: e.tensor_tensor(out=p3[:], in0=b0[:], in1=sin_, op=ALU.mult), reads=[kb[0], ck], writes=[kp[3]])
                pe_pair(2, 3, pp[q], (+1, -1), kp)

            def stage_s(j):
                q = j % 2
                kv = [("v32", q, 0), ("v32", q, 1)]
                kv16 = [("v16", q, 0), ("v16", q, 1)]
                rcol = s5v[:, V_R, j:j + 1]
                S.op("dve", lambda e: e.tensor_tensor_scan(out=v32[q][0][:], data0=rcol.to_broadcast([128, T]), data1=PS(2),
                                                           initial=s5v[:, V_XLR, j:j + 1], op0=ALU.mult, op1=ALU.add),
                     reads=[("ps", 2), ("xl", j)], writes=[kv[0]])
                S.op("dve", lambda e: e.tensor_tensor_scan(out=v32[q][1][:], data0=rcol.to_broadcast([128, T]), data1=PS(3),
                                                           initial=s5v[:, V_XLI, j:j + 1], op0=ALU.mult, op1=ALU.add),
                     reads=[("ps", 3), ("xl", j)], writes=[kv[1]])
                S.op("act", lambda e: e.copy(out=v16[q][0][:], in_=v32[q][0][:]), reads=[kv[0]], writes=[kv16[0]])
                S.op("act", lambda e: e.copy(out=v16[q][1][:], in_=v32[q][1][:]), reads=[kv[1]], writes=[kv16[1]])
                S.op("act", lambda e: e.copy(out=s5v[:, V_T2, j:j + 1], in_=v32[q][0][:, T - 1:T]), reads=[kv[0]], writes=[("vl", j, 0)])
                S.op("act", lambda e: e.copy(out=s5v[:, V_T3, j:j + 1], in_=v32[q][1][:, T - 1:T]), reads=[kv[1]], writes=[("vl", j, 1)])

            def stage_b(j):
                c, jj, q = j // 4, j % 4, j % 2
                wb_ = s5w[c % 2]
                ct, ck = cs[j % 3], ("cs", j % 3)
                cos_, sin_ = ct[:, 0, :], ct[:, 1, :]
                p0, p1, p2, p3 = pp[q]
                x0, x1 = xri[q]
                kp = [("pp", q, i) for i in range(4)]
                kv16 = [("v16", q, 0), ("v16", q, 1)]
                kx = [("xri", q, 0), ("xri", q, 1)]
                yb = 4 + c % 2
                vr, vi = v16[q]
                S.op("dve", lambda e: e.tensor_tensor(out=p0[:], in0=vr[:], in1=cos_, op=ALU.mult), reads=[kv16[0], ck], writes=[kp[0]])
                S.op("dve", lambda e: e.tensor_tensor(out=p1[:], in0=vi[:], in1=sin_, op=ALU.mult), reads=[kv16[1], ck], writes=[kp[1]])
                S.op("dve", lambda e: e.tensor_tensor(out=p2[:], in0=vr[:], in1=sin_, op=ALU.mult), reads=[kv16[0], ck], writes=[kp[2]])
                S.op("dve", lambda e: e.tensor_tensor(out=p3[:], in0=vi[:], in1=cos_, op=ALU.mult), reads=[kv16[1], ck], writes=[kp[3]])
                pe_pair(6, 7, pp[q], (-1, +1), kp)
                S.op("act", lambda e: e.copy(out=x0[:], in_=PS(6)), reads=[("ps", 6)], writes=[kx[0]])
                S.op("act", lambda e: e.copy(out=x1[:], in_=PS(7)), reads=[("ps", 7)], writes=[kx[1]])
                S.op("pe", lambda e: e.matmul(PS(yb), lhsT=wb_[:, jj * 4 + 2, :], rhs=x0[:], start=(jj == 0), stop=False),
                     reads=[("s5w", c % 2), kx[0]], writes=[("ps", yb)])
                S.op("pe", lambda e: e.matmul(PS(yb), lhsT=wb_[:, jj * 4 + 3, :], rhs=x1[:], start=False, stop=(jj == 3)),
                     reads=[("s5w", c % 2), kx[1]], writes=[("ps", yb)])
                if jj == 3:
                    yt, y2, yk = tmp[0], tmp[1], [("tmp", 0), ("tmp", 1)]
                    S.op("dve", lambda e: e.scalar_tensor_tensor(out=yt[:], in0=ubf[:, c, :], scalar=cv[:, CV_DSK + c:CV_DSK + c + 1], in1=PS(yb),
                                                                 op0=ALU.mult, op1=ALU.add), reads=[("ubf", c), ("ps", yb)], writes=[yk[0]])
                    S.op("act", lambda e: e.activation(out=y2[:], in_=yt[:], func=AF.Square, scale=math.sqrt(GK * 0.044715)), reads=[yk[0]], writes=[yk[1]])
                    S.op("dve", lambda e: e.scalar_tensor_tensor(out=y2[:], in0=y2[:], scalar=GK, in1=yt[:], op0=ALU.add, op1=ALU.mult), reads=yk, writes=[yk[1]])
                    S.op("act", lambda e: e.activation(out=y2[:], in_=y2[:], func=AF.Sigmoid), reads=[yk[1]], writes=[yk[1]])
                    S.op("dve", lambda e: e.tensor_tensor(out=act[:, c, :], in0=yt[:], in1=y2[:], op=ALU.mult), reads=yk, writes=[("act", c)])

            S.op("act", lambda e: e.copy(out=identb[:], in_=ident), writes=["identb"])
            S.op("dve", lambda e: e.tensor_scalar(out=nidentb[:], in0=ident, scalar1=-1.0, scalar2=None, op0=ALU.mult), writes=["nidentb"])
            s5w_load(0)
            tab_load(0)
            stage_pe(0)
            for i in range(NJ + 1):
                if i + 1 < NJ:
                    stage_pe(i + 1)
                if i < NJ:
                    stage_f(i)
                if i >= 1:
                    stage_b(i - 1)
                if i < NJ:
                    stage_s(i)
            VV = lambda i: s5v[:, i, :]
            kvl = [("vl", j, i) for j in range(NJ) for i in range(2)]
            kxl = [("xl", j) for j in range(NJ)]
            S.op("dve", lambda e: e.tensor_tensor(out=VV(V_T0), in0=VV(V_T3), in1=VV(V_SL), op=ALU.mult), reads=kvl, writes=["xlt0"])
            S.op("dve", lambda e: e.tensor_tensor(out=VV(V_T1), in0=VV(V_T3), in1=VV(V_CL), op=ALU.mult), reads=kvl, writes=["xlt1"])
            S.op("dve", lambda e: e.tensor_tensor(out=VV(V_XLR), in0=VV(V_T2), in1=VV(V_CL), op=ALU.mult), reads=kvl, writes=kxl)
            S.op("dve", lambda e: e.tensor_tensor(out=VV(V_XLR), in0=VV(V_XLR), in1=VV(V_T0), op=ALU.subtract), reads=["xlt0"] + kxl, writes=kxl)
            S.op("dve", lambda e: e.tensor_tensor(out=VV(V_XLI), in0=VV(V_T2), in1=VV(V_SL), op=ALU.mult), reads=kvl + kxl, writes=kxl)
            S.op("dve", lambda e: e.tensor_tensor(out=VV(V_XLI), in0=VV(V_XLI), in1=VV(V_T1), op=ALU.add), reads=["xlt1"] + kxl, writes=kxl)
            S.barrier()
            steps = []
            ykeys = [("act", k) for k in range(MC)]
            for oc in range(MC):
                def cgl(tl, oc=oc):
                    b = oc % 4
                    mm_col_tile(tl, lambda k: act[:, k, :], ykeys, b)
                    t = tmp[oc % 2]
                    tk_ = ("tmp", oc % 2)
                    S.op("act", lambda e: e.activation(out=t[:], in_=PS(b), func=AF.Sigmoid, bias=cv[:, CV_BGL + oc:CV_BGL + oc + 1], scale=1.0),
                         reads=[("ps", b)], writes=[tk_])
                    S.op("dve", lambda e: e.tensor_tensor(out=t[:], in0=t[:], in1=act[:, oc, :], op=ALU.mult), reads=[tk_, ("act", oc)], writes=[tk_])
                    stats_add(t[:], [tk_], 6, oc == 0, oc == MC - 1)
                    S.op("dve", lambda e: e.tensor_scalar(out=xn[:, MC + oc, :], in0=t[:], scalar1=cv[:, CV_SG + oc:CV_SG + oc + 1], scalar2=None, op0=ALU.mult),
                         reads=[tk_], writes=[("xn", MC + oc)])
                steps.append((lambda oc=oc: load_col_tile(w_glu_d, 0, MC, oc * 128), cgl, (MC + 15) // 16))
            run_steps(steps, PF=2)
            make_rstd(6, MW, rstdS, "rstdS")
            steps = []
            for dc in range(DC):
                def ld(dc=dc):
                    return (load_col_tile(w_out_d, 0, MC, dc * 128), load_col_tile(w_out_d, MW, MC, dc * 128))

                def co(tl, dc=dc):
                    b1, b2 = (dc % 2) * 2, (dc % 2) * 2 + 1
                    mm_col_tile(tl[0], lambda k: xn[:, k, :], [("xn", k) for k in range(MC)], b1)
                    mm_col_tile(tl[1], lambda k: xn[:, MC + k, :], [("xn", MC + k) for k in range(MC)], b2)
                    t = tmp[2 + dc % 2]
                    tk_ = ("tmp", 2 + dc % 2)
                    S.op("dve", lambda e: e.tensor_tensor(out=t[:], in0=PS(b1), in1=rstdP[:], op=ALU.mult), reads=[("ps", b1), "rstdP"], writes=[tk_])
                    S.op("pool", lambda e: e.tensor_tensor(out=h[:, dc, :], in0=h[:, dc, :], in1=t[:], op=ALU.add), reads=[tk_, ("h", dc)], writes=[("h", dc)])
                    t2 = tmp[dc % 2]
                    tk2 = ("tmp", dc % 2)
                    S.op("dve", lambda e: e.tensor_tensor(out=t2[:], in0=PS(b2), in1=rstdS[:], op=ALU.mult), reads=[("ps", b2), "rstdS"], writes=[tk2])
                    S.op("pool", lambda e: e.tensor_tensor(out=h[:, dc, :], in0=h[:, dc, :], in1=t2[:], op=ALU.add), reads=[tk2, ("h", dc)], writes=[("h", dc)])
                steps.append((ld, co, 2 * ((MC + 15) // 16)))
            run_steps(steps, PF=1)

        def final_store(ti):
            norm_stats_h()
            make_rstd(7, D, rstdA, "rstdA")
            for dc in range(DC):
                S.op("dve", lambda e, dc=dc: e.scalar_tensor_tensor(
                    out=h[:, dc, :], in0=h[:, dc, :], scalar=cv[:, CV_FIN + dc:CV_FIN + dc + 1], in1=rstdA[:],
                    op0=ALU.mult, op1=ALU.mult), reads=[("h", dc), "rstdA"], writes=[("h", dc)])
            toks = []
            for tb in range(TB):
                sl = tb % len(iost)
                ost = iost[sl]
                for q4 in range(DC // 4):
                    b = q4 % 4

                    def f(e, q4=q4, b=b, tb=tb):
                        r = None
                        for i in range(4):
                            r = e.transpose(out=ps[b][:, i * 128:(i + 1) * 128],
                                            in_=h[:, 4 * q4 + i, tb * 128:(tb + 1) * 128], identity=ident)
                        return r
                    S.op("pe", f, reads=[("h", 4 * q4 + i) for i in range(4)], writes=[("ps", b)])
                    eng = "act" if q4 % 2 == 0 else "dve"
                    S.op(eng, copy_on(eng, ost[:, q4 * 512:(q4 + 1) * 512], ps[b][:, 0:512]), reads=[("ps", b)], writes=[("io", sl)])
                r0 = ti * T + tb * 128
                toks.append(S.dma("sp", "io%d" % sl, lambda q, sem, ost=ost, r0=r0: q.dma_start(out=out_d[r0:r0 + 128, :], in_=ost[:]).then_inc(sem, 16),
                                  reads=[("io", sl)], writes=[("out", ti, tb)]))
            return toks

        setup()
        for ti in range(NT):
            load_x(ti)
            S.barrier()
            norm_h(CV_F1)
            ns_active["n"] = NS if ti == 0 else NS + 2
            ffn(w["ffn1_gate"], w["ffn1_up"], w["ffn1_down"], bgl=(bg if ti == 0 else None))
            ns_active["n"] = NS
            S.barrier()
            mixer(ti)
            S.barrier()
            norm_h(CV_F2)
            ns_active["n"] = NS + 2
            ffn(w["ffn2_gate"], w["ffn2_up"], w["ffn2_down"])
            ns_active["n"] = NS
            S.barrier()
            final_store(ti)
            if ti == NT - 1:
                S.barrier()
        S.run_block()
    return nc


def prep_inputs(cfg, inp):
    f = lambda a: np.ascontiguousarray(np.asarray(a, dtype=np.float32))
    D, DC, MC, NG, NJ = cfg.D, cfg.DC, cfg.MC, cfg.NG, cfg.NJ
    col = lambda v: f(v).reshape(-1, 128).T
    cvec = np.concatenate([col(inp["ffn1_norm"]), col(inp["mix_norm"]), col(inp["ffn2_norm"]), col(inp["final_norm"]),
                           col(inp["pool_scale"]), col(inp["pool_out_norm"]), col(inp["ssm_out_norm"]),
                           col(inp["d_skip"]), col(inp["b_glu"])], axis=1)
    st = lambda a: f(a).reshape(NJ, 2, 64).transpose(1, 2, 0).reshape(128, NJ)
    ldt = np.repeat(f(inp["log_dt"])[:, None], 64, axis=1)
    s5p = np.stack([st(inp["lam_re"]), st(inp["lam_im"]), st(ldt)], axis=1)
    bz = np.zeros((128, NJ, 2, 128), np.float32)
    cz = np.zeros((128, NJ, 2, 128), np.float32)
    for ri, (bn, cn) in enumerate((("b_re", "c_re"), ("b_im", "c_im"))):
        b = f(inp[bn])
        c = f(inp[cn])
        for g in range(NG):
            j, g2, g8 = g // 2, g % 2, g % 8
            bz[g8 * 16:(g8 + 1) * 16, j, ri, g2 * 64:(g2 + 1) * 64] = b[g].T
            cz[g2 * 64:(g2 + 1) * 64, j, ri, g8 * 16:(g8 + 1) * 16] = c[g].T
    cmat = np.stack([np.eye(128, dtype=np.float32), np.ones((128, 128), np.float32)], axis=1)
    iota = np.broadcast_to(np.arange(1, cfg.T + 1, dtype=np.float32)[None, :], (128, cfg.T)).copy()
    icnt = np.zeros((128, 4, 16), np.float32)
    for wi, wn in enumerate(POOL_WINDOWS):
        icnt[:, wi, :] = 1.0 / np.minimum(np.arange(16) + 1, wn).astype(np.float32)
    shared = {
        "cvec": f(cvec), "s5p": f(s5p), "bz": bz, "cz": cz, "cmat": f(cmat), "iota": iota, "icnt": icnt,
        "w_in": f(inp["w_in"]), "w_out": f(inp["w_out"]), "w_glu": f(inp["w_glu"]),
        "w_pool": f(inp["w_pool"]).reshape(4 * cfg.PGW, cfg.PGW),
    }
    for n in ("ffn1_gate", "ffn1_up", "ffn1_down", "ffn2_gate", "ffn2_up", "ffn2_down"):
        shared[n] = f(inp[n])
    x = f(inp["x"])
    return [dict(shared, x=x[b]) for b in range(cfg.NCORES)]


_CACHE = {}


def run(cfg, inputs):
    in_maps = prep_inputs(cfg, inputs)
    key = (cfg.D, cfg.S, cfg.T, cfg.NCORES)
    if key not in _CACHE:
        _CACHE[key] = build_program(cfg)
    nc = _CACHE[key]
    res = run_bass_kernel_spmd(nc, in_maps, core_ids=list(range(cfg.NCORES)))
    return np.stack([np.asarray(r["out"], dtype=np.float32) for r in res.results], axis=0)


def kernel(**inputs):
    cfg = Cfg()
    return run(cfg, inputs)
```

```python
import math
import numpy as np
from contextlib import ExitStack
import concourse.bass as bass
import concourse.mybir as mybir
from concourse.bass_utils import run_bass_kernel_spmd

F32 = mybir.dt.float32
BF16 = mybir.dt.bfloat16
I32 = mybir.dt.int32
AF = mybir.ActivationFunctionType
ALU = mybir.AluOpType

ENGS = ["pe", "act", "dve", "pool", "sp"]
POOL_WINDOWS = (2, 4, 8, 16)
NORM_EPS = 1e-6


class Cfg:
    def __init__(self, D=4096, S=2048, T=512, NCORES=8):
        self.D, self.S, self.T, self.NCORES = D, S, T, NCORES
        self.DFF = ((8 * D // 3 + 255) // 256) * 256
        self.DC = D // 128
        self.FC = self.DFF // 128
        self.MW = D // 2
        self.MC = self.MW // 128
        self.NG = self.MW // 16
        self.NJ = self.NG // 2
        self.PGW = self.MW // 4
        self.PGC = self.PGW // 128
        self.NT = S // T
        self.TB = T // 128
        self.NCV = 4 * self.DC + 5 * self.MC


class Sched:
    def __init__(self, nc, stack):
        self.nc = nc
        self.stack = stack
        self.sem = {e: stack.enter_context(nc.semaphore("s_" + e)) for e in ENGS}
        self.cnt = {e: 0 for e in ENGS}
        self.ops = {e: [] for e in ENGS}
        self.seen = {e: {} for e in ENGS}
        self.last_w = {}
        self.readers = {}
        self.dsem = {}

    def _deps(self, reads, writes):
        toks = []
        for k in reads:
            t = self.last_w.get(k)
            if t is not None:
                toks.append(t)
        for k in writes:
            t = self.last_w.get(k)
            if t is not None:
                toks.append(t)
            toks.extend(self.readers.get(k, ()))
        return toks

    def _emit_waits(self, eng, toks):
        need = {}
        for sk, v in toks:
            if sk == "pe" and eng == "pe":
                continue
            if v > need.get(sk, 0):
                need[sk] = v
        seen = self.seen[eng]
        for sk, v in need.items():
            if seen.get(sk, 0) >= v:
                continue
            seen[sk] = v
            self.ops[eng].append(("w", sk, v))

    def _commit(self, tok, reads, writes):
        for k in reads:
            self.readers.setdefault(k, []).append(tok)
        for k in writes:
            self.last_w[k] = tok
            self.readers[k] = []

    def op(self, eng, fn, reads=(), writes=()):
        self._emit_waits(eng, self._deps(reads, writes))
        self.cnt[eng] += 1
        tok = (eng, self.cnt[eng])
        self.ops[eng].append(("o", fn))
        self._commit(tok, reads, writes)
        return tok

    def dma(self, q, slot, fn, reads=(), writes=(), n=1):
        if slot not in self.dsem:
            self.dsem[slot] = [self.stack.enter_context(self.nc.semaphore("d_" + slot)), 0]
        self._emit_waits(q, self._deps(reads, writes))
        self.dsem[slot][1] += 16 * n
        tok = ("D" + slot, self.dsem[slot][1])
        self.ops[q].append(("d", fn, slot))
        self._commit(tok, reads, writes)
        return tok

    def all_tokens(self):
        toks = [(e, self.cnt[e]) for e in ENGS if self.cnt[e] > 0]
        toks += [("D" + s, v[1]) for s, v in self.dsem.items() if v[1] > 0]
        return toks

    def barrier(self):
        toks = self.all_tokens()
        for e in ENGS:
            self._emit_waits(e, toks)
        self.last_w.clear()
        self.readers.clear()

    def _semh(self, sk):
        if sk.startswith("D"):
            return self.dsem[sk[1:]][0]
        return self.sem[sk]

    def replay(self, eng, handle):
        sem = self.sem[eng]
        for o in self.ops[eng]:
            if o[0] == "w":
                handle.wait_ge(self._semh(o[1]), o[2])
            elif o[0] == "o":
                o[1](handle).then_inc(sem, 1)
            else:
                o[1](handle, self.dsem[o[2]][0])

    def run_block(self):
        with self.nc.Block() as block:
            @block.tensor
            def _(e):
                self.replay("pe", e)

            @block.scalar
            def _(e):
                self.replay("act", e)

            @block.vector
            def _(e):
                self.replay("dve", e)

            @block.gpsimd
            def _(e):
                self.replay("pool", e)

            @block.sync
            def _(e):
                self.replay("sp", e)


def build_program(cfg):
    D, DFF, S_, T = cfg.D, cfg.DFF, cfg.S, cfg.T
    DC, FC, MW, MC, NG, NJ, PGC, NT, TB = cfg.DC, cfg.FC, cfg.MW, cfg.MC, cfg.NG, cfg.NJ, cfg.PGC, cfg.NT, cfg.TB
    KH = DC // 2
    nc = bass.Bass("TRN2", target_bir_lowering=False)

    def din(name, shape, dt=F32):
        return nc.dram_tensor(name, list(shape), dt, kind="ExternalInput").ap()

    x_d = din("x", [S_, D])
    w = {}
    for f in ("ffn1", "ffn2"):
        w[f + "_gate"] = din(f + "_gate", [D, DFF])
        w[f + "_up"] = din(f + "_up", [D, DFF])
        w[f + "_down"] = din(f + "_down", [DFF, D])
    w_in_d = din("w_in", [D, D])
    w_out_d = din("w_out", [D, D])
    w_glu_d = din("w_glu", [MW, MW])
    w_pool_d = din("w_pool", [4 * cfg.PGW, cfg.PGW])
    cvec_d = din("cvec", [128, cfg.NCV])
    s5p_d = din("s5p", [128, 3, NJ])
    bz_d = din("bz", [128, NJ, 2, 128])
    cz_d = din("cz", [128, NJ, 2, 128])
    cmat_d = din("cmat", [128, 2, 128])
    iota_d = din("iota", [128, T])
    icnt_d = din("icnt", [128, 4, 16])
    out_d = nc.dram_tensor("out", [S_, D], F32, kind="ExternalOutput").ap()
    tab_d = nc.dram_tensor("tab_scr", [NJ, 128, 2, T], BF16).ap()
    s5w_d = nc.dram_tensor("s5w_scr", [MC, 128, 16, 128], BF16).ap()

    CV_F1, CV_MIX, CV_F2, CV_FIN = 0, DC, 2 * DC, 3 * DC
    CV_PSC = 4 * DC
    CV_PG, CV_SG, CV_DSK, CV_BGL = CV_PSC + MC, CV_PSC + 2 * MC, CV_PSC + 3 * MC, CV_PSC + 4 * MC

    with ExitStack() as st:
        S = Sched(nc, st)
        sb_state = {"off": (nc.sbuf_base + 63) // 64 * 64, "n": 0}

        def alloc(shape, dt, at=None):
            esz = 4 if dt in (F32, I32) else 2
            nbytes = int(np.prod(shape[1:])) * esz
            nbytes = (nbytes + 63) // 64 * 64
            if at is None:
                off = sb_state["off"]
                sb_state["off"] += nbytes
                assert sb_state["off"] <= nc.sbuf_top, ("SBUF overflow", sb_state["off"], nc.sbuf_top)
            else:
                off = at
            sb_state["n"] += 1
            sb_state["last"] = off
            return nc.alloc_sbuf_tensor_at("t%d" % sb_state["n"], list(shape), dt, offset=off)

        h = alloc([128, DC, T], F32)
        xn = alloc([128, DC, T], BF16)
        ACTC = max(16, MC)
        act = alloc([128, ACTC, T], BF16)
        ubf = alloc([128, max(16, MC), T], BF16)
        ubf_off = sb_state["last"]
        cv = alloc([128, cfg.NCV], F32)
        cmat = alloc([128, 2, 128], F32)
        ident = cmat[:, 0, :]
        ones = cmat[:, 1, :]
        icnt = alloc([128, 4, 16], F32)
        s5p = alloc([128, 3, NJ], F32)
        s5v = alloc([128, 14, NJ], F32)
        V_DT, V_R, V_TH, V_GRE, V_GIM, V_NGIM, V_XLR, V_XLI, V_T0, V_T1, V_T2, V_T3, V_CL, V_SL = range(14)
        halo = alloc([128, MC, 16], F32)
        kcol = alloc([128, 2], F32)
        rstdA = alloc([128, T], F32)
        rstdP = alloc([128, T], F32)
        rstdS = alloc([128, T], F32)
        NTMP = 6
        tmp = []
        tmp_off = []
        for _ in range(NTMP):
            tmp.append(alloc([128, T], F32))
            tmp_off.append(sb_state["last"])
        zt = alloc([128, 16 + T], F32)
        zs = [alloc([128, 16 + T], F32) for _ in range(2)]
        NS, NB = 3, 6
        arena0 = sb_state["off"]
        stage = [alloc([128, 2048], F32) for _ in range(NS)]
        wbf = [alloc([128, 2048], BF16) for _ in range(NB)]
        arena_end = sb_state["off"]
        stage = stage + [alloc([128, 2048], F32, at=ubf_off), alloc([128, 2048], F32, at=ubf_off + 8192)]
        ns_active = {"n": NS}
        ARENA = arena_end - arena0
        ao = {"o": arena0}

        def aalloc(shape, dt):
            esz = 4 if dt in (F32, I32) else 2
            nbytes = (int(np.prod(shape[1:])) * esz + 63) // 64 * 64
            t = alloc(shape, dt, at=ao["o"])
            ao["o"] += nbytes
            assert ao["o"] <= arena_end, "arena overflow"
            return t

        cs = [aalloc([128, 2, T], BF16) for _ in range(3)]
        bb = [[aalloc([128, T], BF16) for _ in range(2)] for _ in range(2)]
        pp = [[aalloc([128, T], BF16) for _ in range(4)] for _ in range(2)]
        identb = aalloc([128, 128], BF16)
        nidentb = aalloc([128, 128], BF16)
        v32 = [[aalloc([128, T], F32) for _ in range(2)] for _ in range(2)]
        v16 = [[aalloc([128, T], BF16) for _ in range(2)] for _ in range(2)]
        xri = [[aalloc([128, T], BF16) for _ in range(2)] for _ in range(2)]
        s5w = [aalloc([128, 16, 128], BF16) for _ in range(2)]
        ao["o"] = arena0
        iost = [aalloc([128, D], F32) for _ in range(2)] if 2 * D * 4 <= ARENA else [aalloc([128, D], F32)]
        bzc = alloc([128, 4, 2, 128], F32, at=ubf_off)
        czc = alloc([128, 4, 2, 128], F32, at=ubf_off + 4096)
        s5o = alloc([128, 4, 4, 128], BF16, at=ubf_off + 8192)
        tab_ta = alloc([128, 128], F32, at=ubf_off + 12288)
        tab_tb = alloc([128, 128], F32, at=ubf_off + 12288 + 512)
        ki0 = alloc([128, T], I32, at=tmp_off[2])
        rbuf = [zt, rstdP]
        csb = alloc([128, 2, T], BF16, at=ubf_off + 13312)
        assert 13312 + 4 * T <= 2 * max(16, MC) * T
        print('SBUF used', sb_state['off'], 'of', nc.sbuf_top, 'arena', ARENA, flush=True)
        ps = [st.enter_context(nc.psum_tensor("ps%d" % i, [128, 512], F32)) for i in range(8)]

        def PS(b):
            return ps[b][:, 0:T]

        rr = {"cast": 0, "st": 0, "wb": 0, "ev": 0}
        CAST_ENGS = ["act", "dve", "act", "dve", "act"]

        def copy_on(eng, out, in_):
            if eng == "act":
                return lambda e: e.copy(out=out, in_=in_)
            return lambda e: e.tensor_copy(out=out, in_=in_)

        def wload(W, r0, kc, c0, ncol, ceng=None):
            assert kc * ncol <= 2048
            s = rr["st"] % ns_active["n"]
            rr["st"] += 1
            b = rr["wb"] % NB
            rr["wb"] += 1
            src = W[r0:r0 + kc * 128, c0:c0 + ncol].rearrange("(k p) n -> p k n", p=128)
            dst = stage[s][:, 0:kc * ncol].rearrange("p (k n) -> p k n", n=ncol)
            S.dma("sp", "st%d" % s, lambda q, sem: q.dma_start(out=dst, in_=src).then_inc(sem, 16),
                  writes=[("st", s)])
            if ceng is None:
                eng = CAST_ENGS[rr["cast"] % len(CAST_ENGS)]
                rr["cast"] += 1
            else:
                eng = ceng
            S.op(eng, copy_on(eng, wbf[b][:, 0:kc * ncol], stage[s][:, 0:kc * ncol]),
                 reads=[("st", s)], writes=[("wb", b)])
            return (wbf[b][:, 0:kc * ncol].rearrange("p (k n) -> p k n", n=ncol), ("wb", b))

        def run_steps(steps, PF=2, bgl=None):
            n = len(steps)
            nsl = [(st_[2] if len(st_) > 2 else 2) for st_ in steps]
            stride = max(1, (n - 8) // max(1, len(bgl))) if bgl else 0
            loaded = {}
            nxt = 0
            inflight = 0
            def prefetch():
                nonlocal nxt, inflight
                while nxt < n and (inflight + nsl[nxt] <= NB):
                    loaded[nxt] = steps[nxt][0]()
                    inflight += nsl[nxt]
                    nxt += 1
            while nxt < min(n, 2):
                loaded[nxt] = steps[nxt][0]()
                inflight += nsl[nxt]
                nxt += 1
            for k in range(n):
                assert k in loaded
                steps[k][1](loaded.pop(k))
                inflight -= nsl[k]
                prefetch()
                if bgl and k % stride == 0:
                    bgl.pop(0)()
            while bgl:
                bgl.pop(0)()

        def load_col_tile(W, r0, nk, c0):
            tiles = []
            k = 0
            while k < nk:
                kc = min(16, nk - k)
                v, key = wload(W, r0 + k * 128, kc, c0, 128)
                tiles.append((v, key, kc))
                k += kc
            return tiles

        def mm_col_tile(tiles, rhs_fn, rhs_keys, bank, first=True, last=True):
            def f(e):
                r = None
                kk = 0
                nk = sum(t[2] for t in tiles)
                for (v, key, kc) in tiles:
                    for k in range(kc):
                        r = e.matmul(PS(bank), lhsT=v[:, k, :], rhs=rhs_fn(kk),
                                     start=(first and kk == 0), stop=(last and kk == nk - 1))
                        kk += 1
                return r
            S.op("pe", f, reads=[t[1] for t in tiles] + list(rhs_keys), writes=[("ps", bank)])

        def make_rstd(bank, N, dst, dkey):
            S.op("dve", lambda e: e.tensor_scalar(out=dst[:], in0=PS(bank), scalar1=1.0 / N, scalar2=NORM_EPS,
                                                  op0=ALU.mult, op1=ALU.add),
                 reads=[("ps", bank)], writes=[dkey])
            S.op("act", lambda e: e.activation(out=dst[:], in_=dst[:], func=AF.Sqrt), reads=[dkey], writes=[dkey])
            S.op("dve", lambda e: e.reciprocal(out=dst[:], in_=dst[:]), reads=[dkey], writes=[dkey])

        sqi = {"i": 0}

        def stats_add(src_ap, src_keys, bank, first, last, scale=None):
            i = sqi["i"] % 2
            sqi["i"] += 1
            sq = tmp[4 + i]
            if scale is None:
                S.op("act", lambda e: e.activation(out=sq[:], in_=src_ap, func=AF.Square),
                     reads=src_keys, writes=[("tmp", 4 + i)])
            else:
                S.op("act", lambda e: e.activation(out=sq[:], in_=src_ap, func=AF.Square, scale=scale),
                     reads=src_keys, writes=[("tmp", 4 + i)])
            S.op("pe", lambda e: e.matmul(PS(bank), lhsT=ones, rhs=sq[:], start=first, stop=last),
                 reads=[("tmp", 4 + i)], writes=[("ps", bank)])

        def norm_stats_h():
            for dc in range(DC):
                if dc % 2 == 0:
                    stats_add(h[:, dc, :], [("h", dc)], 7, dc == 0, dc == DC - 1)
                else:
                    i = 2 + (dc // 2) % 2
                    sq = tmp[i]
                    S.op("dve", lambda e, dc=dc, sq=sq: e.tensor_tensor(out=sq[:], in0=h[:, dc, :], in1=h[:, dc, :], op=ALU.mult),
                         reads=[("h", dc)], writes=[("tmp", i)])
                    S.op("pe", lambda e, dc=dc, sq=sq: e.matmul(PS(7), lhsT=ones, rhs=sq[:], start=(dc == 0), stop=(dc == DC - 1)),
                         reads=[("tmp", i)], writes=[("ps", 7)])

        def norm_h(gcol):
            sqb = [2, 4, 5]
            for dc in range(DC):
                i = sqb[dc % 3]
                sq = tmp[i]
                S.op("dve", lambda e, dc=dc, sq=sq: e.tensor_tensor(out=sq[:], in0=h[:, dc, :], in1=h[:, dc, :], op=ALU.mult),
                     reads=[("h", dc)], writes=[("tmp", i)])
                S.op("pe", lambda e, dc=dc, sq=sq: e.matmul(PS(7), lhsT=ones, rhs=sq[:], start=(dc == 0), stop=(dc == DC - 1)),
                     reads=[("tmp", i)], writes=[("ps", 7)])
            for dc in range(DC):
                S.op("act", lambda e, dc=dc: e.activation(out=xn[:, dc, :], in_=h[:, dc, :], func=AF.Identity,
                                                          scale=cv[:, gcol + dc:gcol + dc + 1]),
                     reads=[("h", dc)], writes=[("xn", dc)])
            make_rstd(7, D, rstdA, "rstdA")

        def load_x(ti):
            for tb in range(TB):
                sl = tb % len(iost)
                xin = iost[sl]
                r0 = ti * T + tb * 128
                S.dma("sp", "io%d" % sl, lambda q, sem, xin=xin, r0=r0: q.dma_start(out=xin[:], in_=x_d[r0:r0 + 128, :]).then_inc(sem, 16),
                      writes=[("io", sl)])
                for q4 in range(DC // 4):
                    b = q4 % 4
                    def f(e, q4=q4, b=b, xin=xin):
                        r = None
                        for i in range(4):
                            r = e.transpose(out=ps[b][:, i * 128:(i + 1) * 128],
                                            in_=xin[:, (4 * q4 + i) * 128:(4 * q4 + i + 1) * 128], identity=ident)
                        return r
                    S.op("pe", f, reads=[("io", sl), "c1"], writes=[("ps", b)])
                    eng = "act" if q4 % 2 == 0 else "dve"
                    S.op(eng, copy_on(eng, h[:, 4 * q4:4 * q4 + 4, tb * 128:(tb + 1) * 128],
                                      ps[b][:, 0:512].rearrange("p (a n) -> p a n", n=128)),
                         reads=[("ps", b)], writes=[("h", 4 * q4 + i) for i in range(4)])

        def ffn(wg, wu, wd, bgl=None):
            groups = [list(range(g0, min(g0 + 8, FC))) for g0 in range(0, FC, 8)]
            gu_steps, d_steps = [], []
            cnt = {"gu": 0, "d": 0}
            for gi, grp in enumerate(groups):
                ab = (gi % 2) * 8
                gu_steps.append([])
                d_steps.append([])
                for li, fc in enumerate(grp):
                    gb = cnt["gu"] % 2
                    ub = 2 + cnt["gu"] % 2
                    cnt["gu"] += 1
                    xkeys = [("xn", k) for k in range(DC)]

                    def cg(tl, gb=gb):
                        mm_col_tile(tl, lambda k: xn[:, k, :], xkeys, gb)

                    def cu(tl, gb=gb, ub=ub, ai=ab + li):
                        mm_col_tile(tl, lambda k: xn[:, k, :], xkeys, ub)
                        ti_ = gb
                        tu_ = 4 + gb
                        S.op("dve", lambda e: e.tensor_tensor(out=tmp[ti_][:], in0=PS(gb), in1=rstdA[:], op=ALU.mult),
                             reads=[("ps", gb), "rstdA"], writes=[("tmp", ti_)])
                        S.op("act", lambda e: e.activation(out=tmp[ti_][:], in_=tmp[ti_][:], func=AF.Silu),
                             reads=[("tmp", ti_)], writes=[("tmp", ti_)])
                        S.op("dve", lambda e: e.tensor_tensor(out=tmp[tu_][:], in0=PS(ub), in1=rstdA[:], op=ALU.mult),
                             reads=[("ps", ub), "rstdA"], writes=[("tmp", tu_)])
                        S.op("dve", lambda e: e.tensor_tensor(out=act[:, ai, :], in0=tmp[ti_][:], in1=tmp[tu_][:], op=ALU.mult),
                             reads=[("tmp", ti_), ("tmp", tu_)], writes=[("act", ai)])
                    gu_steps[gi].append((lambda fc=fc: load_col_tile(wg, 0, DC, fc * 128), cg, (DC + 15) // 16))
                    gu_steps[gi].append((lambda fc=fc: load_col_tile(wu, 0, DC, fc * 128), cu, (DC + 15) // 16))
                ng = len(grp)
                for dp in range(DC // 2):
                    def ld(grp=grp, ng=ng, dp=dp):
                        return wload(wd, grp[0] * 128, ng, dp * 256, 256)

                    def cd(tl, ng=ng, dp=dp, ab=ab):
                        v, key = tl
                        for i in range(2):
                            b = 4 + cnt["d"] % 4
                            cnt["d"] += 1
                            dc = 2 * dp + i

                            def f(e, b=b, i=i):
                                r = None
                                for k in range(ng):
                                    r = e.matmul(PS(b), lhsT=v[:, k, i * 128:(i + 1) * 128], rhs=act[:, ab + k, :],
                                                 start=(k == 0), stop=(k == ng - 1))
                                return r
                            S.op("pe", f, reads=[key] + [("act", ab + k) for k in range(ng)], writes=[("ps", b)])
                            S.op("dve", lambda e, b=b, dc=dc: e.scalar_tensor_tensor(
                                out=h[:, dc, :], in0=PS(b), scalar=0.5, in1=h[:, dc, :], op0=ALU.mult, op1=ALU.add),
                                reads=[("ps", b), ("h", dc)], writes=[("h", dc)])
                    d_steps[gi].append((ld, cd, 1))
            steps = list(gu_steps[0])
            for gi in range(len(groups)):
                if gi + 1 < len(groups):
                    steps += gu_steps[gi + 1]
                steps += d_steps[gi]
            run_steps(steps, PF=2, bgl=bgl)

        def range_reduce(ph, phk, kint, kf, kfk, r, rk):
            C1 = 6.28125
            C2 = 2.0 * math.pi - C1
            S.op("dve", lambda e: e.tensor_scalar(out=kf, in0=ph, scalar1=1.0 / (2.0 * math.pi), scalar2=None, op0=ALU.mult),
                 reads=[phk], writes=[kfk])
            S.op("dve", lambda e: e.tensor_copy(out=kint, in_=kf), reads=[kfk], writes=[("tmp", 2)])
            S.op("dve", lambda e: e.tensor_copy(out=kf, in_=kint), reads=[("tmp", 2)], writes=[kfk])
            S.op("dve", lambda e: e.scalar_tensor_tensor(out=r, in0=kf, scalar=-C1, in1=ph, op0=ALU.mult, op1=ALU.add),
                 reads=[kfk, phk], writes=[rk])
            S.op("dve", lambda e: e.scalar_tensor_tensor(out=r, in0=kf, scalar=-C2, in1=r, op0=ALU.mult, op1=ALU.add),
                 reads=[kfk, rk], writes=[rk])
            S.op("dve", lambda e: e.tensor_scalar(out=r, in0=r, scalar1=-3.1415925, scalar2=3.1415925, op0=ALU.max, op1=ALU.min),
                 reads=[rk], writes=[rk])

        def sincos(r, rk, na, nak, out_sin, sk, out_cos, ck):
            S.op("act", lambda e: e.activation(out=na, in_=r, func=AF.Abs), reads=[rk], writes=[nak])
            S.op("act", lambda e: e.activation(out=out_sin, in_=r, func=AF.Sin), reads=[rk], writes=[sk])
            S.op("act", lambda e: e.activation(out=out_cos, in_=na, func=AF.Sin, bias=kcol[:, 0:1], scale=-1.0),
                 reads=[nak, "kcol"], writes=[ck])

        bg = []

        def setup():
            for (dst, src, nm) in ((cv, cvec_d, "c0"), (cmat, cmat_d, "c1"), (icnt, icnt_d, "c2"), (s5p, s5p_d, "c3")):
                S.dma("sp", nm, lambda q, sem, dst=dst, src=src: q.dma_start(out=dst[:], in_=src).then_inc(sem, 16),
                      writes=[nm])
            iota = rstdS
            S.dma("sp", "c4", lambda q, sem: q.dma_start(out=iota[:], in_=iota_d).then_inc(sem, 16), writes=["iota"])
            S.op("dve", lambda e: e.memset(halo[:], 0.0), writes=["halo"])
            S.op("dve", lambda e: e.memset(s5v[:], 0.0), writes=["s5v"])
            S.op("dve", lambda e: e.memset(kcol[:, 0:1], math.pi / 2.0), writes=["kcol"])
            lre, lim, ldt = s5p[:, 0, :], s5p[:, 1, :], s5p[:, 2, :]
            V = lambda i: s5v[:, i, :]
            K = "s5v"
            S.op("act", lambda e: e.activation(out=V(V_DT), in_=ldt, func=AF.Exp), reads=["c3", K], writes=[K])
            S.op("dve", lambda e: e.tensor_tensor(out=V(V_T0), in0=lre, in1=V(V_DT), op=ALU.mult), reads=[K], writes=[K])
            S.op("act", lambda e: e.activation(out=V(V_R), in_=V(V_T0), func=AF.Exp), reads=[K], writes=[K])
            S.op("dve", lambda e: e.tensor_tensor(out=V(V_TH), in0=lim, in1=V(V_DT), op=ALU.mult), reads=[K], writes=[K])
            kint_s = ki0[:, 0:NJ]
            range_reduce(V(V_TH), K, kint_s, V(V_T0), K, V(V_T3), K)
            sincos(V(V_T3), K, V(V_T0), K, V(V_T1), K, V(V_T2), K)
            S.op("dve", lambda e: e.tensor_tensor(out=V(V_T2), in0=V(V_T2), in1=V(V_R), op=ALU.mult), reads=[K], writes=[K])
            S.op("dve", lambda e: e.tensor_tensor(out=V(V_T1), in0=V(V_T1), in1=V(V_R), op=ALU.mult), reads=[K], writes=[K])
            S.op("dve", lambda e: e.tensor_scalar(out=V(V_T2), in0=V(V_T2), scalar1=-1.0, scalar2=None, op0=ALU.add), reads=[K], writes=[K])
            S.op("dve", lambda e: e.tensor_tensor(out=V(V_T0), in0=lre, in1=lre, op=ALU.mult), reads=[K], writes=[K])
            S.op("dve", lambda e: e.tensor_tensor(out=V(V_T3), in0=lim, in1=lim, op=ALU.mult), reads=[K], writes=[K])
            S.op("dve", lambda e: e.tensor_tensor(out=V(V_T0), in0=V(V_T0), in1=V(V_T3), op=ALU.add), reads=[K], writes=[K])
            S.op("dve", lambda e: e.reciprocal(out=V(V_T0), in_=V(V_T0)), reads=[K], writes=[K])
            S.op("dve", lambda e: e.tensor_tensor(out=V(V_GRE), in0=V(V_T2), in1=lre, op=ALU.mult), reads=[K], writes=[K])
            S.op("dve", lambda e: e.tensor_tensor(out=V(V_T3), in0=V(V_T1), in1=lim, op=ALU.mult), reads=[K], writes=[K])
            S.op("dve", lambda e: e.tensor_tensor(out=V(V_GRE), in0=V(V_GRE), in1=V(V_T3), op=ALU.add), reads=[K], writes=[K])
            S.op("dve", lambda e: e.tensor_tensor(out=V(V_GRE), in0=V(V_GRE), in1=V(V_T0), op=ALU.mult), reads=[K], writes=[K])
            S.op("dve", lambda e: e.tensor_tensor(out=V(V_GIM), in0=V(V_T1), in1=lre, op=ALU.mult), reads=[K], writes=[K])
            S.op("dve", lambda e: e.tensor_tensor(out=V(V_T3), in0=V(V_T2), in1=lim, op=ALU.mult), reads=[K], writes=[K])
            S.op("dve", lambda e: e.tensor_tensor(out=V(V_GIM), in0=V(V_GIM), in1=V(V_T3), op=ALU.subtract), reads=[K], writes=[K])
            S.op("dve", lambda e: e.tensor_tensor(out=V(V_GIM), in0=V(V_GIM), in1=V(V_T0), op=ALU.mult), reads=[K], writes=[K])
            S.op("dve", lambda e: e.tensor_scalar(out=V(V_NGIM), in0=V(V_GIM), scalar1=-1.0, scalar2=None, op0=ALU.mult), reads=[K], writes=[K])
            S.op("dve", lambda e: e.memset(s5v[:, V_XLR:V_XLI + 1, :], 0.0), reads=[K], writes=[K])
            tabA, tabB, tabC = [], [], []
            for j in range(NJ):
                r_ = rbuf[j % 2][:, 0:T]
                rk = ("rr", j % 2)

                def tA(j=j, r_=r_, rk=rk):
                    ph, kf = zs[0][:, 0:T], zs[1][:, 0:T]
                    S.op("dve", lambda e: e.tensor_scalar(out=ph, in0=iota[:], scalar1=s5v[:, V_TH, j:j + 1], scalar2=None, op0=ALU.mult),
                         reads=["iota", K], writes=["ph"])
                    range_reduce(ph, "ph", ki0[:], kf, "kf", r_, rk)

                def tB(j=j, r_=r_, rk=rk):
                    na = tmp[3]
                    S.op("act", lambda e: e.activation(out=na[:], in_=r_, func=AF.Abs), reads=[rk], writes=[("tmp", 3)])
                    S.op("act", lambda e: e.activation(out=csb[:, 1, :], in_=r_, func=AF.Sin), reads=[rk], writes=["csb"])
                    S.op("act", lambda e: e.activation(out=s5v[:, V_SL, j:j + 1], in_=r_[:, T - 1:T], func=AF.Sin), reads=[rk], writes=[("csl", j)])
                    S.op("act", lambda e: e.activation(out=csb[:, 0, :], in_=na[:], func=AF.Sin, bias=kcol[:, 0:1], scale=-1.0),
                         reads=[("tmp", 3), "kcol"], writes=["csb"])
                    S.op("act", lambda e: e.activation(out=s5v[:, V_CL, j:j + 1], in_=na[:, T - 1:T], func=AF.Sin, bias=kcol[:, 0:1], scale=-1.0),
                         reads=[("tmp", 3), "kcol"], writes=[("csl", j)])

                def tC(j=j):
                    S.dma("sp", "tabc", lambda q, sem: q.dma_start(out=tab_d[j], in_=csb[:]).then_inc(sem, 16),
                          reads=["csb"], writes=[("tab", j)])
                tabA.append(tA)
                tabB.append(tB)
                tabC.append(tC)
            for j in range(NJ + 1):
                def slot_even(j=j):
                    if j >= 1:
                        tabC[j - 1]()
                    if j < NJ:
                        tabA[j]()
                bg.append(slot_even)
                if j < NJ:
                    bg.append(tabB[j])
            m1, m2, m3 = [], [], []
            for c in range(MC):
                def mt1(c=c):
                    S.dma("sp", "bzc", lambda q, sem: q.dma_start(out=bzc[:], in_=bz_d[:, 4 * c:4 * c + 4, :, :]).then_inc(sem, 16), writes=["bzc"])
                    S.dma("sp", "czc", lambda q, sem: q.dma_start(out=czc[:], in_=cz_d[:, 4 * c:4 * c + 4, :, :]).then_inc(sem, 16), writes=["czc"])

                def mt2(c=c):
                    S.op("act", lambda e: e.copy(out=s5o[:, :, 0:2, :], in_=bzc[:]), reads=["bzc"], writes=["s5o"])
                    for jj in range(4):
                        j = 4 * c + jj
                        czr, czi = czc[:, jj, 0, :], czc[:, jj, 1, :]
                        ta, tb_ = tab_ta[:], tab_tb[:]
                        S.op("dve", lambda e, j=j, czi=czi: e.tensor_scalar(out=ta, in0=czi, scalar1=s5v[:, V_GIM, j:j + 1], scalar2=None, op0=ALU.mult),
                             reads=["czc", K], writes=["ta"])
                        S.op("dve", lambda e, j=j, czr=czr, jj=jj: e.scalar_tensor_tensor(out=s5o[:, jj, 2, :], in0=czr, scalar=s5v[:, V_GRE, j:j + 1], in1=ta,
                                                                                         op0=ALU.mult, op1=ALU.subtract),
                             reads=["czc", K, "ta"], writes=["s5o"])
                        S.op("dve", lambda e, j=j, czi=czi: e.tensor_scalar(out=tb_, in0=czi, scalar1=s5v[:, V_GRE, j:j + 1], scalar2=None, op0=ALU.mult),
                             reads=["czc", K], writes=["tb"])
                        S.op("dve", lambda e, j=j, czr=czr, jj=jj: e.scalar_tensor_tensor(out=s5o[:, jj, 3, :], in0=czr, scalar=s5v[:, V_NGIM, j:j + 1], in1=tb_,
                                                                                         op0=ALU.mult, op1=ALU.subtract),
                             reads=["czc", K, "tb"], writes=["s5o"])

                def mt3(c=c):
                    S.dma("sp", "s5ow", lambda q, sem: q.dma_start(out=s5w_d[c], in_=s5o[:].rearrange("p a b n -> p (a b) n")).then_inc(sem, 16),
                          reads=["s5o"], writes=[("s5wd", c)])
                m1.append(mt1)
                m2.append(mt2)
                m3.append(mt3)
            for c in range(MC + 1):
                def mslot(c=c):
                    if c >= 1:
                        m3[c - 1]()
                    if c < MC:
                        m1[c]()
                bg.append(mslot)
                if c < MC:
                    bg.append(m2[c])

        def mixer(ti):
            norm_h(CV_MIX)
            xkeys = [("xn", k) for k in range(DC)]
            steps = []
            for oc in range(2 * MC):
                def cin(tl, oc=oc):
                    b = oc % 4
                    mm_col_tile(tl, lambda k: xn[:, k, :], xkeys, b)
                    if oc >= MC:
                        c = oc - MC
                        S.op("dve", lambda e: e.tensor_tensor(out=ubf[:, c, :], in0=PS(b), in1=rstdA[:], op=ALU.mult),
                             reads=[("ps", b), "rstdA"], writes=[("ubf", c)])
                        return
                    c = oc
                    wi = c // PGC
                    win = POOL_WINDOWS[wi]
                    S.op("pool", lambda e: e.tensor_copy(out=zt[:, 0:16], in_=halo[:, c, :]), reads=[("halo", c)], writes=["zt"])
                    S.op("dve", lambda e: e.tensor_tensor(out=zt[:, 16:16 + T], in0=PS(b), in1=rstdA[:], op=ALU.mult),
                         reads=[("ps", b), "rstdA"], writes=["zt"])
                    S.op("pool", lambda e: e.tensor_copy(out=halo[:, c, :], in_=zt[:, T:T + 16]), reads=["zt"], writes=[("halo", c)])
                    src, sk = zt, "zt"
                    lag = 1
                    k = 0
                    while lag < win:
                        dst = zs[k % 2]
                        dk = ("zs", k % 2)
                        lo = 2 * lag - 1
                        S.op("pool", lambda e, src=src, dst=dst, lo=lo, lag=lag: e.tensor_tensor(
                            out=dst[:, lo:16 + T], in0=src[:, lo:16 + T], in1=src[:, lo - lag:16 + T - lag], op=ALU.add),
                            reads=[sk], writes=[dk])
                        src, sk = dst, dk
                        lag *= 2
                        k += 1
                    S.op("dve", lambda e, src=src: e.scalar_tensor_tensor(
                        out=act[:, c, :], in0=src[:, 16:16 + T], scalar=1.0 / win, in1=zt[:, 16:16 + T],
                        op0=ALU.mult, op1=ALU.subtract), reads=[sk, "zt"], writes=[("act", c)])
                    if ti == 0:
                        S.op("dve", lambda e, src=src: e.tensor_tensor(out=tmp[2][:, 0:16], in0=src[:, 16:32], in1=icnt[:, wi, :], op=ALU.mult),
                             reads=[sk], writes=[("tmp", 2)])
                        S.op("dve", lambda e: e.tensor_tensor(out=act[:, c, 0:16], in0=tmp[2][:, 0:16], in1=zt[:, 16:32], op=ALU.subtract),
                             reads=[("tmp", 2), "zt"], writes=[("act", c)])
                steps.append((lambda oc=oc: load_col_tile(w_in_d, 0, DC, oc * 128), cin, (DC + 15) // 16))
            run_steps(steps, PF=2)
            steps = []
            for g in range(4):
                for o in range(PGC):
                    c = g * PGC + o

                    def cp(tl, g=g, c=c):
                        b = 4 + c % 2
                        mm_col_tile(tl, lambda k: act[:, g * PGC + k, :], [("act", g * PGC + k) for k in range(PGC)], b)
                        t = tmp[c % 2]
                        S.op("act", lambda e: e.activation(out=t[:], in_=PS(b), func=AF.Identity, scale=cv[:, CV_PSC + c:CV_PSC + c + 1]),
                             reads=[("ps", b)], writes=[("tmp", c % 2)])
                        stats_add(t[:], [("tmp", c % 2)], 7, c == 0, c == MC - 1)
                        S.op("dve", lambda e: e.tensor_scalar(out=xn[:, c, :], in0=t[:], scalar1=cv[:, CV_PG + c:CV_PG + c + 1], scalar2=None, op0=ALU.mult),
                             reads=[("tmp", c % 2)], writes=[("xn", c)])
                    steps.append((lambda g=g, o=o: load_col_tile(w_pool_d, g * cfg.PGW, PGC, o * 128), cp, 1))
            run_steps(steps, PF=2)
            make_rstd(7, MW, rstdP, "rstdP")
            S.barrier()
            GK = 2.0 * math.sqrt(2.0 / math.pi)

            def s5w_load(c):
                wb_ = s5w[c % 2]
                S.dma("sp", "s5w%d" % (c % 2), lambda q, sem: q.dma_start(out=wb_[:], in_=s5w_d[c]).then_inc(sem, 16),
                      reads=[("s5wd", c)], writes=[("s5w", c % 2)])

            def tab_load(j):
                ct = cs[j % 3]
                S.dma("sp", "cs%d" % (j % 3), lambda q, sem: q.dma_start(out=ct[:], in_=tab_d[j]).then_inc(sem, 16),
                      reads=[("tab", j)], writes=[("cs", j % 3)])

            def stage_pe(j):
                c, jj, q = j // 4, j % 4, j % 2
                wb_ = s5w[c % 2]
                S.op("pe", lambda e: e.matmul(PS(0), lhsT=wb_[:, jj * 4 + 0, :], rhs=ubf[:, c, :], start=True, stop=True),
                     reads=[("s5w", c % 2), ("ubf", c)], writes=[("ps", 0)])
                S.op("pe", lambda e: e.matmul(PS(1), lhsT=wb_[:, jj * 4 + 1, :], rhs=ubf[:, c, :], start=True, stop=True),
                     reads=[("s5w", c % 2), ("ubf", c)], writes=[("ps", 1)])
                b0, b1 = bb[q]
                S.op("act", lambda e: e.copy(out=b0[:], in_=PS(0)), reads=[("ps", 0)], writes=[("bb", q, 0)])
                S.op("act", lambda e: e.copy(out=b1[:], in_=PS(1)), reads=[("ps", 1)], writes=[("bb", q, 1)])

            def pe_pair(bank_a, bank_b, ps_, signs, kp, extra=()):
                def f(e):
                    e.matmul(PS(bank_a), lhsT=identb[:], rhs=ps_[0][:], start=True, stop=False)
                    e.matmul(PS(bank_a), lhsT=(identb if signs[0] > 0 else nidentb)[:], rhs=ps_[1][:], start=False, stop=True)
                    e.matmul(PS(bank_b), lhsT=identb[:], rhs=ps_[2][:], start=True, stop=False)
                    return e.matmul(PS(bank_b), lhsT=(identb if signs[1] > 0 else nidentb)[:], rhs=ps_[3][:], start=False, stop=True)
                S.op("pe", f, reads=list(kp) + ["identb", "nidentb"] + list(extra), writes=[("ps", bank_a), ("ps", bank_b)])

            def stage_f(j):
                q = j % 2
                if j % 4 == 1 and j // 4 + 1 < MC:
                    s5w_load(j // 4 + 1)
                if j + 1 < NJ:
                    tab_load(j + 1)
                ct, ck = cs[j % 3], ("cs", j % 3)
                cos_, sin_ = ct[:, 0, :], ct[:, 1, :]
                b0, b1 = bb[q]
                p0, p1, p2, p3 = pp[q]
                kb = [("bb", q, 0), ("bb", q, 1)]
                kp = [("pp", q, i) for i in range(4)]
                S.op("dve", lambda e: e.tensor_tensor(out=p0[:], in0=b0[:], in1=cos_, op=ALU.mult), reads=[kb[0], ck], writes=[kp[0]])
                S.op("dve", lambda e: e.tensor_tensor(out=p1[:], in0=b1[:], in1=sin_, op=ALU.mult), reads=[kb[1], ck], writes=[kp[1]])
                S.op("dve", lambda e: e.tensor_tensor(out=p2[:], in0=b1[:], in1=cos_, op=ALU.mult), reads=[kb[1], ck], writes=[kp[2]])
                S.op("dve", lambda e: e.tensor_tensor(out=p3[:], in0=b0[:], in1=sin_, op=ALU.mult), reads=[kb[0], ck], writes=[kp[3]])
                pe_pair(2, 3, pp[q], (+1, -1), kp)

            def stage_s(j):
                q = j % 2
                kv = [("v32", q, 0), ("v32", q, 1)]
                kv16 = [("v16", q, 0), ("v16", q, 1)]
                rcol = s5v[:, V_R, j:j + 1]
                S.op("dve", lambda e: e.tensor_tensor_scan(out=v32[q][0][:], data0=rcol.to_broadcast([128, T]), data1=PS(2),
                                                           initial=s5v[:, V_XLR, j:j + 1], op0=ALU.mult, op1=ALU.add),
                     reads=[("ps", 2), ("xl", j)], writes=[kv[0]])
                S.op("dve", lambda e: e.tensor_tensor_scan(out=v32[q][1][:], data0=rcol.to_broadcast([128, T]), data1=PS(3),
                                                           initial=s5v[:, V_XLI, j:j + 1], op0=ALU.mult, op1=ALU.add),
                     reads=[("ps", 3), ("xl", j)], writes=[kv[1]])
                S.op("act", lambda e: e.copy(out=v16[q][0][:], in_=v32[q][0][:]), reads=[kv[0]], writes=[kv16[0]])
                S.op("act", lambda e: e.copy(out=v16[q][1][:], in_=v32[q][1][:]), reads=[kv[1]], writes=[kv16[1]])
                S.op("act", lambda e: e.copy(out=s5v[:, V_T2, j:j + 1], in_=v32[q][0][:, T - 1:T]), reads=[kv[0]], writes=[("vl", j, 0)])
                S.op("act", lambda e: e.copy(out=s5v[:, V_T3, j:j + 1], in_=v32[q][1][:, T - 1:T]), reads=[kv[1]], writes=[("vl", j, 1)])

            def stage_b(j):
                c, jj, q = j // 4, j % 4, j % 2
                wb_ = s5w[c % 2]
                ct, ck = cs[j % 3], ("cs", j % 3)
                cos_, sin_ = ct[:, 0, :], ct[:, 1, :]
                p0, p1, p2, p3 = pp[q]
                x0, x1 = xri[q]
                kp = [("pp", q, i) for i in range(4)]
                kv16 = [("v16", q, 0), ("v16", q, 1)]
                kx = [("xri", q, 0), ("xri", q, 1)]
                yb = 4 + c % 2
                vr, vi = v16[q]
                S.op("dve", lambda e: e.tensor_tensor(out=p0[:], in0=vr[:], in1=cos_, op=ALU.mult), reads=[kv16[0], ck], writes=[kp[0]])
                S.op("dve", lambda e: e.tensor_tensor(out=p1[:], in0=vi[:], in1=sin_, op=ALU.mult), reads=[kv16[1], ck], writes=[kp[1]])
                S.op("dve", lambda e: e.tensor_tensor(out=p2[:], in0=vr[:], in1=sin_, op=ALU.mult), reads=[kv16[0], ck], writes=[kp[2]])
                S.op("dve", lambda e: e.tensor_tensor(out=p3[:], in0=vi[:], in1=cos_, op=ALU.mult), reads=[kv16[1], ck], writes=[kp[3]])
                pe_pair(6, 7, pp[q], (-1, +1), kp)
                S.op("act", lambda e: e.copy(out=x0[:], in_=PS(6)), reads=[("ps", 6)], writes=[kx[0]])
                S.op("act", lambda e: e.copy(out=x1[:], in_=PS(7)), reads=[("ps", 7)], writes=[kx[1]])
                S.op("pe", lambda e: e.matmul(PS(yb), lhsT=wb_[:, jj * 4 + 2, :], rhs=x0[:], start=(jj == 0), stop=False),
                     reads=[("s5w", c % 2), kx[0]], writes=[("ps", yb)])
                S.op("pe", lambda e: e.matmul(PS(yb), lhsT=wb_[:, jj * 4 + 3, :], rhs=x1[:], start=False, stop=(jj == 3)),
                     reads=[("s5w", c % 2), kx[1]], writes=[("ps", yb)])
                if jj == 3:
                    yt, y2, yk = tmp[0], tmp[1], [("tmp", 0), ("tmp", 1)]
                    S.op("dve", lambda e: e.scalar_tensor_tensor(out=yt[:], in0=ubf[:, c, :], scalar=cv[:, CV_DSK + c:CV_DSK + c + 1], in1=PS(yb),
                                                                 op0=ALU.mult, op1=ALU.add), reads=[("ubf", c), ("ps", yb)], writes=[yk[0]])
                    S.op("act", lambda e: e.activation(out=y2[:], in_=yt[:], func=AF.Square, scale=math.sqrt(GK * 0.044715)), reads=[yk[0]], writes=[yk[1]])
                    S.op("dve", lambda e: e.scalar_tensor_tensor(out=y2[:], in0=y2[:], scalar=GK, in1=yt[:], op0=ALU.add, op1=ALU.mult), reads=yk, writes=[yk[1]])
                    S.op("act", lambda e: e.activation(out=y2[:], in_=y2[:], func=AF.Sigmoid), reads=[yk[1]], writes=[yk[1]])
                    S.op("dve", lambda e: e.tensor_tensor(out=act[:, c, :], in0=yt[:], in1=y2[:], op=ALU.mult), reads=yk, writes=[("act", c)])

            S.op("act", lambda e: e.copy(out=identb[:], in_=ident), writes=["identb"])
            S.op("dve", lambda e: e.tensor_scalar(out=nidentb[:], in0=ident, scalar1=-1.0, scalar2=None, op0=ALU.mult), writes=["nidentb"])
            s5w_load(0)
            tab_load(0)
            stage_pe(0)
            for i in range(NJ + 1):
                if i + 1 < NJ:
                    stage_pe(i + 1)
                if i < NJ:
                    stage_f(i)
                if i >= 1:
                    stage_b(i - 1)
                if i < NJ:
                    stage_s(i)
            VV = lambda i: s5v[:, i, :]
            kvl = [("vl", j, i) for j in range(NJ) for i in range(2)]
            kxl = [("xl", j) for j in range(NJ)]
            S.op("dve", lambda e: e.tensor_tensor(out=VV(V_T0), in0=VV(V_T3), in1=VV(V_SL), op=ALU.mult), reads=kvl, writes=["xlt0"])
            S.op("dve", lambda e: e.tensor_tensor(out=VV(V_T1), in0=VV(V_T3), in1=VV(V_CL), op=ALU.mult), reads=kvl, writes=["xlt1"])
            S.op("dve", lambda e: e.tensor_tensor(out=VV(V_XLR), in0=VV(V_T2), in1=VV(V_CL), op=ALU.mult), reads=kvl, writes=kxl)
            S.op("dve", lambda e: e.tensor_tensor(out=VV(V_XLR), in0=VV(V_XLR), in1=VV(V_T0), op=ALU.subtract), reads=["xlt0"] + kxl, writes=kxl)
            S.op("dve", lambda e: e.tensor_tensor(out=VV(V_XLI), in0=VV(V_T2), in1=VV(V_SL), op=ALU.mult), reads=kvl + kxl, writes=kxl)
            S.op("dve", lambda e: e.tensor_tensor(out=VV(V_XLI), in0=VV(V_XLI), in1=VV(V_T1), op=ALU.add), reads=["xlt1"] + kxl, writes=kxl)
            S.barrier()
            ns_active["n"] = NS + 2
            steps = []
            ykeys = [("act", k) for k in range(MC)]
            for oc in range(MC):
                def cgl(tl, oc=oc):
                    b = oc % 4
                    mm_col_tile(tl, lambda k: act[:, k, :], ykeys, b)
                    t = tmp[oc % 2]
                    tk_ = ("tmp", oc % 2)
                    S.op("act", lambda e: e.activation(out=t[:], in_=PS(b), func=AF.Sigmoid, bias=cv[:, CV_BGL + oc:CV_BGL + oc + 1], scale=1.0),
                         reads=[("ps", b)], writes=[tk_])
                    S.op("dve", lambda e: e.tensor_tensor(out=t[:], in0=t[:], in1=act[:, oc, :], op=ALU.mult), reads=[tk_, ("act", oc)], writes=[tk_])
                    stats_add(t[:], [tk_], 6, oc == 0, oc == MC - 1)
                    S.op("dve", lambda e: e.tensor_scalar(out=xn[:, MC + oc, :], in0=t[:], scalar1=cv[:, CV_SG + oc:CV_SG + oc + 1], scalar2=None, op0=ALU.mult),
                         reads=[tk_], writes=[("xn", MC + oc)])
                steps.append((lambda oc=oc: load_col_tile(w_glu_d, 0, MC, oc * 128), cgl, (MC + 15) // 16))
            run_steps(steps, PF=2)
            make_rstd(6, MW, rstdS, "rstdS")
            steps = []
            for dc in range(DC):
                def ld(dc=dc):
                    return (load_col_tile(w_out_d, 0, MC, dc * 128), load_col_tile(w_out_d, MW, MC, dc * 128))

                def co(tl, dc=dc):
                    b1, b2 = (dc % 2) * 2, (dc % 2) * 2 + 1
                    mm_col_tile(tl[0], lambda k: xn[:, k, :], [("xn", k) for k in range(MC)], b1)
                    mm_col_tile(tl[1], lambda k: xn[:, MC + k, :], [("xn", MC + k) for k in range(MC)], b2)
                    t = tmp[2 + dc % 2]
                    tk_ = ("tmp", 2 + dc % 2)
                    S.op("dve", lambda e: e.tensor_tensor(out=t[:], in0=PS(b1), in1=rstdP[:], op=ALU.mult), reads=[("ps", b1), "rstdP"], writes=[tk_])
                    S.op("pool", lambda e: e.tensor_tensor(out=h[:, dc, :], in0=h[:, dc, :], in1=t[:], op=ALU.add), reads=[tk_, ("h", dc)], writes=[("h", dc)])
                    t2 = tmp[dc % 2]
                    tk2 = ("tmp", dc % 2)
                    S.op("dve", lambda e: e.tensor_tensor(out=t2[:], in0=PS(b2), in1=rstdS[:], op=ALU.mult), reads=[("ps", b2), "rstdS"], writes=[tk2])
                    S.op("pool", lambda e: e.tensor_tensor(out=h[:, dc, :], in0=h[:, dc, :], in1=t2[:], op=ALU.add), reads=[tk2, ("h", dc)], writes=[("h", dc)])
                steps.append((ld, co, 2 * ((MC + 15) // 16)))
            run_steps(steps, PF=1)
            ns_active["n"] = NS

        def final_store(ti):
            norm_stats_h()
            make_rstd(7, D, rstdA, "rstdA")
            for dc in range(DC):
                S.op("dve", lambda e, dc=dc: e.scalar_tensor_tensor(
                    out=h[:, dc, :], in0=h[:, dc, :], scalar=cv[:, CV_FIN + dc:CV_FIN + dc + 1], in1=rstdA[:],
                    op0=ALU.mult, op1=ALU.mult), reads=[("h", dc), "rstdA"], writes=[("h", dc)])
            toks = []
            for tb in range(TB):
                sl = tb % len(iost)
                ost = iost[sl]
                for q4 in range(DC // 4):
                    b = q4 % 4

                    def f(e, q4=q4, b=b, tb=tb):
                        r = None
                        for i in range(4):
                            r = e.transpose(out=ps[b][:, i * 128:(i + 1) * 128],
                                            in_=h[:, 4 * q4 + i, tb * 128:(tb + 1) * 128], identity=ident)
                        return r
                    S.op("pe", f, reads=[("h", 4 * q4 + i) for i in range(4)], writes=[("ps", b)])
                    eng = "act" if q4 % 2 == 0 else "dve"
                    S.op(eng, copy_on(eng, ost[:, q4 * 512:(q4 + 1) * 512], ps[b][:, 0:512]), reads=[("ps", b)], writes=[("io", sl)])
                r0 = ti * T + tb * 128
                toks.append(S.dma("sp", "io%d" % sl, lambda q, sem, ost=ost, r0=r0: q.dma_start(out=out_d[r0:r0 + 128, :], in_=ost[:]).then_inc(sem, 16),
                                  reads=[("io", sl)], writes=[("out", ti, tb)]))
            return toks

        setup()
        for ti in range(NT):
            load_x(ti)
            S.barrier()
            norm_h(CV_F1)
            ns_active["n"] = NS if ti == 0 else NS + 2
            ffn(w["ffn1_gate"], w["ffn1_up"], w["ffn1_down"], bgl=(bg if ti == 0 else None))
            ns_active["n"] = NS
            S.barrier()
            mixer(ti)
            S.barrier()
            norm_h(CV_F2)
            ns_active["n"] = NS + 2
            ffn(w["ffn2_gate"], w["ffn2_up"], w["ffn2_down"])
            ns_active["n"] = NS
            S.barrier()
            final_store(ti)
            if ti == NT - 1:
                S.barrier()
        S.run_block()
    return nc


def prep_inputs(cfg, inp):
    f = lambda a: np.ascontiguousarray(np.asarray(a, dtype=np.float32))
    D, DC, MC, NG, NJ = cfg.D, cfg.DC, cfg.MC, cfg.NG, cfg.NJ
    col = lambda v: f(v).reshape(-1, 128).T
    cvec = np.concatenate([col(inp["ffn1_norm"]), col(inp["mix_norm"]), col(inp["ffn2_norm"]), col(inp["final_norm"]),
                           col(inp["pool_scale"]), col(inp["pool_out_norm"]), col(inp["ssm_out_norm"]),
                           col(inp["d_skip"]), col(inp["b_glu"])], axis=1)
    st = lambda a: f(a).reshape(NJ, 2, 64).transpose(1, 2, 0).reshape(128, NJ)
    ldt = np.repeat(f(inp["log_dt"])[:, None], 64, axis=1)
    s5p = np.stack([st(inp["lam_re"]), st(inp["lam_im"]), st(ldt)], axis=1)
    bz = np.zeros((128, NJ, 2, 128), np.float32)
    cz = np.zeros((128, NJ, 2, 128), np.float32)
    for ri, (bn, cn) in enumerate((("b_re", "c_re"), ("b_im", "c_im"))):
        b = f(inp[bn])
        c = f(inp[cn])
        for g in range(NG):
            j, g2, g8 = g // 2, g % 2, g % 8
            bz[g8 * 16:(g8 + 1) * 16, j, ri, g2 * 64:(g2 + 1) * 64] = b[g].T
            cz[g2 * 64:(g2 + 1) * 64, j, ri, g8 * 16:(g8 + 1) * 16] = c[g].T
    cmat = np.stack([np.eye(128, dtype=np.float32), np.ones((128, 128), np.float32)], axis=1)
    iota = np.broadcast_to(np.arange(1, cfg.T + 1, dtype=np.float32)[None, :], (128, cfg.T)).copy()
    icnt = np.zeros((128, 4, 16), np.float32)
    for wi, wn in enumerate(POOL_WINDOWS):
        icnt[:, wi, :] = 1.0 / np.minimum(np.arange(16) + 1, wn).astype(np.float32)
    shared = {
        "cvec": f(cvec), "s5p": f(s5p), "bz": bz, "cz": cz, "cmat": f(cmat), "iota": iota, "icnt": icnt,
        "w_in": f(inp["w_in"]), "w_out": f(inp["w_out"]), "w_glu": f(inp["w_glu"]),
        "w_pool": f(inp["w_pool"]).reshape(4 * cfg.PGW, cfg.PGW),
    }
    for n in ("ffn1_gate", "ffn1_up", "ffn1_down", "ffn2_gate", "ffn2_up", "ffn2_down"):
        shared[n] = f(inp[n])
    x = f(inp["x"])
    return [dict(shared, x=x[b]) for b in range(cfg.NCORES)]


_CACHE = {}


def run(cfg, inputs):
    in_maps = prep_inputs(cfg, inputs)
    key = (cfg.D, cfg.S, cfg.T, cfg.NCORES)
    if key not in _CACHE:
        _CACHE[key] = build_program(cfg)
    nc = _CACHE[key]
    res = run_bass_kernel_spmd(nc, in_maps, core_ids=list(range(cfg.NCORES)))
    return np.stack([np.asarray(r["out"], dtype=np.float32) for r in res.results], axis=0)


def kernel(**inputs):
    cfg = Cfg()
    return run(cfg, inputs)
```
